# Optimizing a Trainium2 kernel written in Bass

```python
import jax, jax.numpy as jnp
from jax import lax
import numpy as np


D_MODEL = 1024
BATCH = 8
SEQ = 4096
DEPTH = 2

CHUNK = 64
EPS = 1e-6
PLE_DIM = 256
D_FF = 2816
FFN_RES = 0.5
N_EVEN = (DEPTH + 1) // 2
N_ODD = DEPTH // 2

GLA_HEADS = 4
GLA_V = D_MODEL // 2
GLA_DV = GLA_V // GLA_HEADS
GLA_QK = GLA_V // 2
GLA_DK = GLA_QK // GLA_HEADS
GLA_RANK = 16
GLA_GATE_TEMP = 16.0

LRU_WIDTH = D_MODEL // 2
LRU_BLOCKS = 8
LRU_BW = LRU_WIDTH // LRU_BLOCKS
CONV_W = 4
LRU_C = 8.0

MIX0_SPLITS = [int(c) for c in np.cumsum([GLA_QK, GLA_QK, GLA_V, GLA_V, GLA_RANK, LRU_WIDTH])]
MIX0_COLS = 2 * GLA_QK + 2 * GLA_V + GLA_RANK + 2 * LRU_WIDTH
MIX0_OUT = GLA_V + LRU_WIDTH

HG_DK = 128
HG_HEADS = D_MODEL // HG_DK
HG_DV = 128
HG_K = HG_HEADS * HG_DK
HG_V = HG_HEADS * HG_DV
MIX1_SPLITS = [HG_K, 2 * HG_K, 2 * HG_K + HG_V]
MIX1_COLS = 2 * HG_K + 2 * HG_V

kernel_name = 'hybrid_gla_rglru_hgrn2_macaron_ple'


def rmsnorm(x, g):
    xf = x.astype(jnp.float32)
    y = xf * lax.rsqrt(jnp.mean(xf * xf, axis=-1, keepdims=True) + EPS)
    return (y * g.astype(jnp.float32)).astype(x.dtype)


def swiglu(h, w1, w3, w2):
    return (jax.nn.silu(h @ w1) * (h @ w3)) @ w2


def chunk_end_gated_linear_attention(q, k, v, log_f):
    b, s, h, dk = q.shape
    dv = v.shape[-1]
    n = s // CHUNK

    def chunks(t):
        return jnp.moveaxis(t.astype(jnp.float32).reshape(b, n, CHUNK, h, t.shape[-1]), 1, 0)

    qc, kc, vc, lf = chunks(q), chunks(k), chunks(v), chunks(log_f)
    cum = jnp.cumsum(lf, axis=2)
    total = cum[:, :, -1]
    k_dec = kc * jnp.exp(total[:, :, None] - cum)
    g_chunk = jnp.exp(total)

    def step(state, xs):
        q_c, k_c, v_c, g_c = xs
        state = g_c[..., None] * state + jnp.einsum('bchk,bchv->bhkv', k_c, v_c)
        return state, jnp.einsum('bchk,bhkv->bchv', q_c, state)

    s0 = jnp.zeros((b, h, dk, dv), jnp.float32)
    _, o = lax.scan(step, s0, (qc, k_dec, vc, g_chunk))
    return jnp.moveaxis(o, 0, 1).reshape(b, s, h, dv).astype(v.dtype)


def causal_depthwise_conv(x, w, bias):
    s = x.shape[1]
    xp = jnp.pad(x, ((0, 0), (CONV_W - 1, 0), (0, 0)))
    out = xp[:, 0:s] * w[0]
    for j in range(1, CONV_W):
        out = out + xp[:, j:j + s] * w[j]
    return out + bias


def block_diag(x, w, bias):
    b, s, wd = x.shape
    xb = x.reshape(b, s, LRU_BLOCKS, LRU_BW)
    return jnp.einsum('bsgi,gij->bsgj', xb, w).reshape(b, s, wd) + bias


def rg_lru(x, w_a, b_a, w_x, b_x, lam):
    r = jax.nn.sigmoid(block_diag(x, w_a, b_a)).astype(jnp.float32)
    i = jax.nn.sigmoid(block_diag(x, w_x, b_x))
    log_a = LRU_C * r * jax.nn.log_sigmoid(lam.astype(jnp.float32))
    a = jnp.exp(log_a)
    u = jnp.sqrt(-jnp.expm1(2.0 * log_a)) * (i * x).astype(jnp.float32)

    def combine(left, right):
        a1, b1 = left
        a2, b2 = right
        return a1 * a2, a2 * b1 + b2

    _, hs = lax.associative_scan(combine, (a, u), axis=1)
    return hs.astype(x.dtype)


def mixer_gla_rglru(h, w_in, gla_gate_up, gla_gate_bias, gla_head_norm, conv_w, conv_b,
                    lru_wa, lru_ba, lru_wx, lru_bx, lru_lambda, w_out):
    b, s, _ = h.shape
    z = h @ w_in
    q, k, v, g, lr, xr, xg = jnp.split(z, MIX0_SPLITS, axis=-1)
    log_f = jax.nn.log_sigmoid((lr @ gla_gate_up + gla_gate_bias).astype(jnp.float32)) / GLA_GATE_TEMP
    o = chunk_end_gated_linear_attention(
        q.reshape(b, s, GLA_HEADS, GLA_DK) * (GLA_DK ** -0.5),
        k.reshape(b, s, GLA_HEADS, GLA_DK),
        v.reshape(b, s, GLA_HEADS, GLA_DV),
        log_f.reshape(b, s, GLA_HEADS, GLA_DK))
    o = rmsnorm(o, gla_head_norm).reshape(b, s, GLA_V) * jax.nn.silu(g)
    y = rg_lru(causal_depthwise_conv(xr, conv_w, conv_b), lru_wa, lru_ba, lru_wx, lru_bx, lru_lambda)
    y = y * jax.nn.gelu(xg)
    return jnp.concatenate([o, y], axis=-1) @ w_out


def mixer_hgrn2(h, w_in, lb, hg_head_norm, w_out):
    b, s, _ = h.shape
    z = h @ w_in
    q, fz, i, g = jnp.split(z, MIX1_SPLITS, axis=-1)
    lbf = lb.astype(jnp.float32)
    zf = fz.astype(jnp.float32)
    log_f = jnp.logaddexp(jnp.log(lbf), jnp.log1p(-lbf) + jax.nn.log_sigmoid(zf))
    k = (1.0 - lbf) * jax.nn.sigmoid(-zf)
    o = chunk_end_gated_linear_attention(
        jax.nn.silu(q).reshape(b, s, HG_HEADS, HG_DK),
        k.astype(h.dtype).reshape(b, s, HG_HEADS, HG_DK),
        i.reshape(b, s, HG_HEADS, HG_DV),
        log_f.reshape(b, s, HG_HEADS, HG_DK))
    o = rmsnorm(o, hg_head_norm).reshape(b, s, HG_V) * jax.nn.silu(g)
    return o @ w_out


def lower_bounds(lb_logits):
    sm = jax.nn.softmax(lb_logits.astype(jnp.float32), axis=0)
    return jnp.cumsum(sm, axis=0) - sm[0]


def half_ffn(x, g, w1, w3, w2):
    return x + FFN_RES * swiglu(rmsnorm(x, g), w1, w3, w2)


def setup_inputs(seed: int = 0) -> dict:
    key = jax.random.key(seed)
    ks = iter(list(jax.random.split(key, 40)))

    def nrm(shape, scale):
        return jax.random.normal(next(ks), shape, jnp.float32) * scale

    def gain(shape):
        return 1.0 + nrm(shape, 0.05)

    a0 = jax.random.uniform(next(ks), (N_EVEN, LRU_WIDTH), jnp.float32, 0.9, 0.999)
    return {
        'x': nrm((BATCH, SEQ, D_MODEL), 1.0),
        'p': nrm((DEPTH, BATCH, SEQ, PLE_DIM), 1.0),
        'ffn_norm': gain((DEPTH, 2, D_MODEL)),
        'ffn_w1': nrm((DEPTH, 2, D_MODEL, D_FF), D_MODEL ** -0.5),
        'ffn_w3': nrm((DEPTH, 2, D_MODEL, D_FF), D_MODEL ** -0.5),
        'ffn_w2': nrm((DEPTH, 2, D_FF, D_MODEL), D_FF ** -0.5),
        'mix_norm': gain((DEPTH, D_MODEL)),
        'ple_norm': gain((DEPTH, D_MODEL)),
        'ple_proj': nrm((DEPTH, PLE_DIM, D_MODEL), PLE_DIM ** -0.5),
        'ple_gate': nrm((DEPTH, D_MODEL, D_MODEL), D_MODEL ** -0.5),
        'final_norm': gain((D_MODEL,)),
        'm0_w_in': nrm((N_EVEN, D_MODEL, MIX0_COLS), D_MODEL ** -0.5),
        'gla_gate_up': nrm((N_EVEN, GLA_RANK, GLA_QK), GLA_RANK ** -0.5),
        'gla_gate_bias': nrm((N_EVEN, GLA_QK), 0.1),
        'gla_head_norm': gain((N_EVEN, GLA_DV)),
        'lru_conv_w': nrm((N_EVEN, CONV_W, LRU_WIDTH), CONV_W ** -0.5),
        'lru_conv_b': nrm((N_EVEN, LRU_WIDTH), 0.02),
        'lru_wa': nrm((N_EVEN, LRU_BLOCKS, LRU_BW, LRU_BW), LRU_BW ** -0.5),
        'lru_ba': nrm((N_EVEN, LRU_WIDTH), 0.02),
        'lru_wx': nrm((N_EVEN, LRU_BLOCKS, LRU_BW, LRU_BW), LRU_BW ** -0.5),
        'lru_bx': nrm((N_EVEN, LRU_WIDTH), 0.02),
        'lru_lambda': jnp.log(a0) - jnp.log1p(-a0),
        'm0_w_out': nrm((N_EVEN, MIX0_OUT, D_MODEL), MIX0_OUT ** -0.5),
        'm1_w_in': nrm((N_ODD, D_MODEL, MIX1_COLS), D_MODEL ** -0.5),
        'hgrn_lb_logits': nrm((DEPTH, HG_K), 0.1),
        'hgrn_head_norm': gain((N_ODD, HG_DV)),
        'm1_w_out': nrm((N_ODD, HG_V, D_MODEL), HG_V ** -0.5),
    }


def reference(x, p, ffn_norm, ffn_w1, ffn_w3, ffn_w2, mix_norm, ple_norm, ple_proj, ple_gate,
              final_norm, m0_w_in, gla_gate_up, gla_gate_bias, gla_head_norm, lru_conv_w,
              lru_conv_b, lru_wa, lru_ba, lru_wx, lru_bx, lru_lambda, m0_w_out, m1_w_in,
              hgrn_lb_logits, hgrn_head_norm, m1_w_out):
    lbs = lower_bounds(hgrn_lb_logits)
    for layer in range(DEPTH):
        x = half_ffn(x, ffn_norm[layer, 0], ffn_w1[layer, 0], ffn_w3[layer, 0], ffn_w2[layer, 0])
        h = rmsnorm(x, mix_norm[layer])
        if layer % 2 == 0:
            e = layer // 2
            x = x + mixer_gla_rglru(h, m0_w_in[e], gla_gate_up[e], gla_gate_bias[e], gla_head_norm[e],
                                    lru_conv_w[e], lru_conv_b[e], lru_wa[e], lru_ba[e], lru_wx[e],
                                    lru_bx[e], lru_lambda[e], m0_w_out[e])
        else:
            o = layer // 2
            x = x + mixer_hgrn2(h, m1_w_in[o], lbs[layer], hgrn_head_norm[o], m1_w_out[o])
        x = half_ffn(x, ffn_norm[layer, 1], ffn_w1[layer, 1], ffn_w3[layer, 1], ffn_w2[layer, 1])
        gate = jax.nn.sigmoid(rmsnorm(x, ple_norm[layer]) @ ple_gate[layer])
        x = x + gate * (p[layer] @ ple_proj[layer])
    return rmsnorm(x, final_norm)
```

```python
import numpy as np
from contextlib import ExitStack
import concourse.bass as bass
import concourse.mybir as mybir
from concourse.bass_utils import run_bass_kernel_spmd

F32 = mybir.dt.float32
BF16 = mybir.dt.bfloat16
AF = mybir.ActivationFunctionType
ALU = mybir.AluOpType

ENGS = ("sync", "act", "pe", "dve", "pool")
D = 1024
SEQ = 4096
DFF = 2816
EPS = 1e-6
M0C = 2576


def _esz(dt):
    return 4 if dt == F32 else 2


class Sched:
    def __init__(self, nc, stack):
        self.nc = nc
        self.stack = stack
        self.streams = {e: [] for e in ENGS}
        self.sem = {}
        self.cnt = {}
        self.seen = {e: {} for e in ENGS}
        self.snap = {}
        self.last_w = {}
        self.readers = {}
        self.pending = {e: ([], []) for e in ENGS}
        self.nwaits = 0
        self.nops = 0

    def _sem(self, v):
        if v not in self.sem:
            self.sem[v] = self.stack.enter_context(self.nc.semaphore("s_" + v))
            self.cnt[v] = 0
        return self.sem[v]

    @staticmethod
    def keys(ap):
        if isinstance(ap, (str, tuple)):
            return [ap]
        name = ap.tensor.name
        pat = ap.ap
        pstride = pat[0][0]
        esz = _esz(ap.dtype)
        off = ap.offset
        p0 = off // pstride
        col = off % pstride
        ext = 1
        for st, c in pat[1:]:
            ext += (c - 1) * abs(st)
        gran = 2048 if name.startswith("PS") else 256
        b0 = (col * esz) // gran
        b1 = ((col + ext) * esz - 1) // gran
        p1 = p0 + pat[0][1] - 1
        halves = set()
        if p0 < 64:
            halves.add(0)
        if p1 >= 64:
            halves.add(1)
        return [(name, b, h) for b in range(b0, b1 + 1) for h in halves]

    def op(self, eng, fn, reads=(), writes=(), slot=None, signal=True):
        rk = [k for a in reads for k in self.keys(a)]
        wk = [k for a in writes for k in self.keys(a)]
        need = {}

        def req(tok):
            if tok is not None and need.get(tok[0], 0) < tok[1]:
                need[tok[0]] = tok[1]

        for k in rk:
            req(self.last_w.get(k))
        for k in wk:
            req(self.last_w.get(k))
            r = self.readers.get(k)
            if r:
                for v, c in r.items():
                    req((v, c))
        if slot is not None:
            pv = self.cnt.get("d_" + slot, 0)
            if pv > 0:
                req(("d_" + slot, pv))
        seen = self.seen[eng]
        waits = []
        for v, c in need.items():
            if c > self.cnt.get(v, 0):
                if v == eng:
                    continue
                print("WARNING future-wait", eng, "on", v, c, self.cnt.get(v, 0))
            if seen.get(v, 0) >= c:
                continue
            waits.append((v, c))
            seen[v] = c
            sn = self.snap.get((v, c))
            if sn:
                for v2, c2 in sn.items():
                    if seen.get(v2, 0) < c2:
                        seen[v2] = c2
        self.nwaits += len(waits)
        self.nops += 1
        veng = eng if slot is None else "d_" + slot
        inc = 1 if slot is None else 16
        self._sem(veng)
        tok = (veng, self.cnt[veng] + inc)
        for k in rk:
            self.readers.setdefault(k, {})[veng] = tok[1]
        for k in wk:
            self.last_w[k] = tok
            self.readers[k] = {}
        if not signal:
            assert slot is None
            self.streams[eng].append((waits, fn, None, 0))
            return
        self.cnt[veng] += inc
        self.snap[tok] = dict(seen)
        self.streams[eng].append((waits, fn, veng, inc))

    def wait_all(self, eng):
        waits = []
        for v, c in self.cnt.items():
            if c > 0 and self.seen[eng].get(v, 0) < c:
                waits.append((v, c))
                self.seen[eng][v] = c
        self.streams[eng].append((waits, None, None, 0))

    def emit(self):
        nc = self.nc
        with nc.Block() as block:
            def run(name, e):
                for waits, fn, veng, inc in self.streams[name]:
                    for v, c in waits:
                        e.wait_ge(self.sem[v], c)
                    if fn is None:
                        continue
                    ins = fn(e)
                    if veng is not None:
                        ins.then_inc(self.sem[veng], inc)

            @block.sync
            def _(e):
                run("sync", e)

            @block.scalar
            def _(e):
                run("act", e)

            @block.tensor
            def _(e):
                run("pe", e)

            @block.vector
            def _(e):
                run("dve", e)

            @block.gpsimd
            def _(e):
                run("pool", e)


VROWS = {}
_r = 0
for _n, _k in (("ffn_norm", 32), ("mix_norm", 16), ("ple_norm", 16), ("final_norm", 8),
               ("gla_head_norm", 1), ("hgrn_head_norm", 1), ("lru_conv_w", 16),
               ("lru_conv_b", 4), ("lru_ba", 4), ("lru_bx", 4), ("lru_lambda", 4)):
    VROWS[_n] = (_r, _k)
    _r += _k
NVROWS = _r

W_SHAPES = {
    "ffn_norm": [2, 2, 1024], "ffn_w1": [2, 2, 1024, 2816], "ffn_w3": [2, 2, 1024, 2816],
    "ffn_w2": [2, 2, 2816, 1024], "mix_norm": [2, 1024], "ple_norm": [2, 1024],
    "ple_proj": [2, 256, 1024], "ple_gate": [2, 1024, 1024], "final_norm": [1024],
    "m0_w_in": [1, 1024, 2576], "gla_gate_up": [1, 16, 256], "gla_gate_bias": [1, 256],
    "gla_head_norm": [1, 128], "lru_conv_w": [1, 4, 512], "lru_conv_b": [1, 512],
    "lru_wa": [1, 8, 64, 64], "lru_ba": [1, 512], "lru_wx": [1, 8, 64, 64], "lru_bx": [1, 512],
    "lru_lambda": [1, 512], "m0_w_out": [1, 1024, 1024], "m1_w_in": [1, 1024, 4096],
    "hgrn_lb_logits": [2, 1024], "hgrn_head_norm": [1, 128], "m1_w_out": [1, 1024, 1024],
}


def make_consts():
    c = np.zeros((128, 3 * 128 + 2), np.float32)
    c[:, 0:128] = np.eye(128, dtype=np.float32)
    c[:, 128:256] = 1.0
    s = np.arange(128)[:, None]
    t = np.arange(128)[None, :]
    c[:, 256:384] = ((s > t) & ((s // 64) == (t // 64))).astype(np.float32)
    c[:, 384] = (np.arange(128) < 64)
    c[:, 385] = (np.arange(128) >= 64)
    return c


class Builder:
    def __init__(self, nc, stack, T=1024, NT=4, stop_after=None, stat_dt=F32):
        self.nc = nc
        self.st = stack
        self.T = T
        self.NT = NT
        self.NH = T // 512
        self.NG = T // 128
        self.stop_after = stop_after
        self.stat_dt = stat_dt
        self.S = Sched(nc, stack)
        self.ws_i = 0
        self.tmp_i = 0
        self.ps_rot = {}

    def view(self, off, shape, dt):
        esz = _esz(dt)
        n = int(np.prod(shape))
        assert off % 4 == 0
        a = self.arena[:, off // 2: off // 2 + n * esz // 2]
        if dt == F32:
            a = a.bitcast(F32)
        if len(shape) == 2:
            a = a.rearrange("p (a b) -> p a b", a=shape[0])
        elif len(shape) == 3:
            a = a.rearrange("p (a b c) -> p a b c", a=shape[0], b=shape[1])
        return a

    def alloc(self, nbytes):
        if nbytes >= 256:
            self.aoff = (self.aoff + 255) // 256 * 256
        off = self.aoff
        self.aoff += (nbytes + 3) // 4 * 4
        return off

    def bank(self, i):
        return self.PS[:, i * 512:(i + 1) * 512]

    def op(self, *a, **k):
        self.S.op(*a, **k)

    def declare(self):
        nc = self.nc
        NTOK = self.T * self.NT
        self.x_d = nc.dram_tensor("x", [NTOK, D], F32, kind="ExternalInput").ap()
        self.p_d = nc.dram_tensor("p", [2, NTOK, 256], F32, kind="ExternalInput").ap()
        self.w = {}
        for n, shp in W_SHAPES.items():
            self.w[n] = nc.dram_tensor(n, shp, F32, kind="ExternalInput").ap()
        self.c_d = nc.dram_tensor("consts", [128, 386], F32, kind="ExternalInput").ap()
        self.y_d = nc.dram_tensor("y", [NTOK, D], F32, kind="ExternalOutput").ap()

    def setup(self):
        nc, st, T = self.nc, self.st, self.T
        ARENA = 212480
        self.arena = st.enter_context(nc.sbuf_tensor("arena", [128, ARENA // 2], BF16))
        self.PS = st.enter_context(nc.psum_tensor("PS", [128, 4096], F32))
        self.aoff = 0
        A = self.alloc
        self.xT = self.view(A(8 * T * 4), [8, T], F32)
        self.hT = self.view(A(8 * T * 2), [8, T], BF16)
        self.R1 = A(22 * T * 2)
        self.W2o = A(22 * T * 2)
        self.RING = 24576
        self.WSbase = A(self.RING)
        self.ring_off = 0
        self.sq = [self.view(A(2048), [512], F32) for _ in range(2)]
        self.rs = self.view(A(2048), [512], F32)
        self.rsb = [self.rs, self.view(A(2048), [512], F32)]
        self.rstd = self.view(A(T * 4), [T], F32)
        self.tmps = [self.view(A(2048), [512], F32) for _ in range(4)]
        self.cst = self.view(A(386 * 4), [386], F32)
        self.ident = self.cst[:, 0:128]
        self.ones = self.cst[:, 128:256]
        self.U = self.cst[:, 256:384]
        self.E = self.cst[:, 384:386]
        self.ones_b = self.view(A(256), [128], BF16)
        self.vstage = self.view(A(512), [128], F32)
        self.vecT = self.view(A(128 * 4), [128], F32)
        self.gbias_b = self.view(A(1024), [256], F32)
        self.lb_b = self.view(A(4096), [1024], F32)
        self.oml_b = self.view(A(4096), [1024], F32)
        self.gup = self.view(A(512), [256], BF16)
        self.bdf = self.view(self.R1, [2, 4, 128], F32)
        self.bd = self.view(A(2048), [2, 4, 128], BF16)
        self.clam = self.view(A(32), [8], F32)
        self.Sst = self.view(A(10 * 512), [10, 128], F32)
        self.Str = self.view(A(8 * 512), [8, 128], F32)
        self.Sb = self.view(A(8 * 256), [8, 128], BF16)
        self.hprev = self.view(A(16), [4], F32)
        self.aoff_end = self.aoff
        assert self.aoff <= ARENA, self.aoff
        R1, W2o = self.R1, self.W2o
        self.hid = self.view(R1, [22, T], BF16)
        self.w2 = self.view(W2o, [22, 1024], BF16)
        self.Xs = self.view(R1, [self.NG, 1024], F32)
        self.Yf = self.view(R1, [8, T], F32)
        self.Ys = self.view(W2o, [self.NG, 1024], F32)
        self.pst = self.view(W2o, [self.NG, 256], F32)
        self.pT = self.view(W2o + self.NG * 1024, [2, T], BF16)
        self.xr_carry = self.view(A(64), [4, 4], F32)

    def tmp(self):
        t = self.tmps[self.tmp_i % 4]
        self.tmp_i += 1
        return t

    def wload(self, src, shape):
        nbytes = int(np.prod(shape)) * 2
        if self.ring_off + nbytes > self.RING:
            self.ring_off = 0
        dst = self.view(self.WSbase + self.ring_off, shape, BF16)
        self.ring_off += nbytes
        name = "ws%d" % (self.ws_i % 8)
        self.ws_i += 1
        self.op("pool", lambda e, dst=dst, src=src: e.dma_start(out=dst, in_=src),
                writes=[dst], slot=name)
        return dst

    def wview(self, wap, c0, c1):
        return wap.rearrange("(k p) n -> p k n", p=128)[:, :, c0:c1]

    def prologue(self):
        op, w = self.op, self.w
        op("sync", lambda e: e.dma_start(out=self.cst, in_=self.c_d), writes=[self.cst], slot="c0")
        op("dve", lambda e: e.memset(self.vstage, 0.0), writes=[self.vstage])
        for i, (n, (r0, k)) in enumerate(VROWS.items()):
            src = w[n]
            shp = W_SHAPES[n]
            if len(shp) == 3:
                src = src.rearrange("a b (c p) -> (a b c) p", p=128)
            elif len(shp) == 2:
                src = src.rearrange("a (c p) -> (a c) p", p=128)
            else:
                src = src.rearrange("(c p) -> c p", p=128)
            op("sync", lambda e, src=src, r0=r0, k=k: e.dma_start(out=self.vstage[r0:r0 + k, :], in_=src),
               writes=[self.vstage], slot="v%d" % i)
        ps = self.bank(0)[:, 0:128]
        op("pe", lambda e: e.transpose(out=ps, in_=self.vstage, identity=self.ident),
           reads=[self.vstage, self.ident], writes=[ps])
        op("dve", lambda e: e.tensor_copy(out=self.vecT, in_=ps), reads=[ps], writes=[self.vecT])
        op("dve", lambda e: e.tensor_copy(out=self.ones_b, in_=self.ones), reads=[self.ones], writes=[self.ones_b])
        op("sync", lambda e: e.dma_start(out=self.gbias_b, in_=w["gla_gate_bias"][0].partition_broadcast(128)),
           writes=[self.gbias_b], slot="b0")
        t01 = self.view(self.W2o, [2, 1024], F32)
        op("sync", lambda e: e.dma_start(out=t01[:, 0, :], in_=w["hgrn_lb_logits"][0].partition_broadcast(128)),
           writes=[t01[:, 0, :]], slot="b1")
        op("sync", lambda e: e.dma_start(out=t01[:, 1, :], in_=w["hgrn_lb_logits"][1].partition_broadcast(128)),
           writes=[t01[:, 1, :]], slot="b2")
        op("dve", lambda e: e.tensor_tensor(out=t01[:, 0, :], in0=t01[:, 1, :], in1=t01[:, 0, :], op=ALU.subtract),
           reads=[t01], writes=[t01[:, 0, :]])
        op("act", lambda e: e.activation(out=self.lb_b, in_=t01[:, 0, :], func=AF.Sigmoid),
           reads=[t01[:, 0, :]], writes=[self.lb_b])
        op("dve", lambda e: e.tensor_scalar(out=self.oml_b, in0=self.lb_b, scalar1=-1.0, scalar2=1.0,
                                            op0=ALU.mult, op1=ALU.add),
           reads=[self.lb_b], writes=[self.oml_b])
        gst = self.tmps[1]
        op("sync", lambda e: e.dma_start(out=gst[0:16, 0:256], in_=w["gla_gate_up"][0]),
           writes=[gst], slot="b3")
        op("dve", lambda e: e.tensor_copy(out=self.gup[0:16, :], in_=gst[0:16, 0:256]), reads=[gst], writes=[self.gup])
        op("dve", lambda e: e.memset(self.bdf, 0.0), writes=[self.bdf])
        for wi, n in enumerate(("lru_wa", "lru_wx")):
            for g in range(8):
                h = g % 2
                dst = self.bdf[h * 64:(h + 1) * 64, wi, g // 2, h * 64:(h + 1) * 64]
                op("sync", lambda e, dst=dst, n=n, g=g: e.dma_start(out=dst, in_=w[n][0, g]),
                   writes=[self.bdf], slot="bd%d_%d" % (wi, g))
        op("dve", lambda e: e.tensor_copy(out=self.bd, in_=self.bdf), reads=[self.bdf], writes=[self.bd])
        r0 = VROWS["lru_lambda"][0]
        lam = self.vecT[:, r0:r0 + 4]
        t = self.tmps[2][:, 0:4]
        op("act", lambda e: e.activation(out=t, in_=lam, func=AF.Exp, scale=-1.0), reads=[self.vecT], writes=[t])
        op("act", lambda e: e.activation(out=t, in_=t, func=AF.Ln, bias=self.onec), reads=[t], writes=[t])
        op("dve", lambda e: e.tensor_scalar(out=self.clam[:, 0:4], in0=t, scalar1=-8.0, scalar2=None, op0=ALU.mult),
           reads=[t], writes=[self.clam])
        op("dve", lambda e: e.tensor_scalar(out=self.clam[:, 4:8], in0=t, scalar1=-16.0, scalar2=None, op0=ALU.mult),
           reads=[t], writes=[self.clam])
        op("dve", lambda e: e.memset(self.Sst, 0.0), writes=[self.Sst])
        op("dve", lambda e: e.memset(self.hprev, 0.0), writes=[self.hprev])
        op("dve", lambda e: e.memset(self.xr_carry, 0.0), writes=[self.xr_carry])

    def vcol(self, name, idx):
        r0, k = VROWS[name]
        assert idx < k
        return self.vecT[:, r0 + idx:r0 + idx + 1]

    def load_x(self, ti):
        op, T = self.op, self.T
        src = self.x_d[ti * T:(ti + 1) * T, :].rearrange("(g p) d -> p g d", p=128)
        for g in range(self.NG):
            op("sync", lambda e, g=g: e.dma_start(out=self.Xs[:, g, :], in_=src[:, g, :]),
               writes=[self.Xs[:, g, :]], slot="xs%d" % g)
        k = 0
        for g in range(self.NG):
            for cq in range(2):
                ps = self.bank(k % 2)
                k += 1
                for c in range(4):
                    cc = cq * 4 + c
                    op("pe", lambda e, ps=ps, g=g, c=c, cc=cc: e.transpose(
                        out=ps[:, c * 128:(c + 1) * 128], in_=self.Xs[:, g, cc * 128:(cc + 1) * 128],
                        identity=self.ident),
                       reads=[self.Xs[:, g, cc * 128:(cc + 1) * 128], self.ident], writes=[ps], signal=(c == 3))
                dst = self.xT[:, cq * 4:(cq + 1) * 4, g * 128:(g + 1) * 128]
                eng = "act" if (k % 2) else "dve"
                if eng == "act":
                    op("act", lambda e, dst=dst, ps=ps: e.copy(out=dst, in_=ps.rearrange("p (c t) -> p c t", c=4)),
                       reads=[ps], writes=[dst])
                else:
                    op("dve", lambda e, dst=dst, ps=ps: e.tensor_copy(out=dst, in_=ps.rearrange("p (c t) -> p c t", c=4)),
                       reads=[ps], writes=[dst])

    def norm(self, gname, gidx0, dst, nfeat_chunks=8, src=None, stat_bank=7):
        op, T = self.op, self.T
        src = self.xT if src is None else src
        sdt = self.stat_dt
        ones = self.ones if sdt == F32 else self.ones_b
        for h in range(self.NH):
            sl = slice(h * 512, (h + 1) * 512)
            ps = self.bank(stat_bank)
            for c in range(8):
                sq = self.sq[c % 2] if sdt == F32 else self.sq[c % 2].bitcast(BF16)[:, 0:512]
                op("act", lambda e, sq=sq, c=c, sl=sl: e.activation(out=sq, in_=src[:, c, sl], func=AF.Square),
                   reads=[src[:, c, sl]], writes=[sq])
                op("pe", lambda e, sq=sq, c=c, ps=ps: e.matmul(ps, lhsT=ones, rhs=sq, start=(c == 0), stop=(c == 7)),
                   reads=[sq, ones], writes=[ps])
            op("act", lambda e, ps=ps: e.activation(out=self.rs, in_=ps, func=AF.Ln, scale=1.0 / D, bias=self.epsc),
               reads=[ps], writes=[self.rs])
            op("act", lambda e, sl=sl: e.activation(out=self.rstd[:, sl], in_=self.rs, func=AF.Exp, scale=-0.5),
               reads=[self.rs], writes=[self.rstd[:, sl]])
            for c in range(8):
                g = self.vcol(gname, gidx0 + c)
                op("dve", lambda e, c=c, sl=sl, g=g: e.scalar_tensor_tensor(
                    out=dst[:, c, sl], in0=src[:, c, sl], scalar=g, in1=self.rstd[:, sl],
                    op0=ALU.mult, op1=ALU.mult),
                   reads=[src[:, c, sl], self.rstd[:, sl], self.vecT], writes=[dst[:, c, sl]])

    def ffn(self, l, i):
        op, T, w = self.op, self.T, self.w
        self.norm("ffn_norm", (l * 2 + i) * 8, self.hT)
        w1 = self.wview(w["ffn_w1"][l, i], 0, DFF)
        w3 = self.wview(w["ffn_w3"][l, i], 0, DFF)
        w2v = w["ffn_w2"][l, i].rearrange("(j p) n -> p j n", p=128)
        blocks = [(b * 256, 256) for b in range(11)]
        w2_parts = [(j, min(2, 22 - j)) for j in range(0, 22, 2)]
        w2_i = 0
        pa = 0
        for bi, (c0, nc_) in enumerate(blocks):
            s1 = self.wload(w1[:, :, c0:c0 + nc_], [8, nc_])
            s3 = self.wload(w3[:, :, c0:c0 + nc_], [8, nc_])
            for _ in range(1):
                if w2_i < len(w2_parts):
                    j0, nj = w2_parts[w2_i]
                    w2_i += 1
                    dst = self.w2[:, j0:j0 + nj, :]
                    op("pool", lambda e, dst=dst, j0=j0, nj=nj: e.dma_start(out=dst, in_=w2v[:, j0:j0 + nj, :]),
                       writes=[dst], slot="w2_%d" % (w2_i - 1))
            for jj in range(nc_ // 128):
                j = c0 // 128 + jj
                for h in range(self.NH):
                    sl = slice(h * 512, (h + 1) * 512)
                    a = self.bank(2 * (pa % 2))
                    b = self.bank(2 * (pa % 2) + 1)
                    pa += 1
                    for k in range(8):
                        op("pe", lambda e, a=a, s1=s1, k=k, jj=jj, sl=sl: e.matmul(
                            a, lhsT=s1[:, k, jj * 128:(jj + 1) * 128], rhs=self.hT[:, k, sl],
                            start=(k == 0), stop=(k == 7)),
                           reads=[s1, self.hT[:, k, sl]], writes=[a], signal=(k == 7))
                    for k in range(8):
                        op("pe", lambda e, b=b, s3=s3, k=k, jj=jj, sl=sl: e.matmul(
                            b, lhsT=s3[:, k, jj * 128:(jj + 1) * 128], rhs=self.hT[:, k, sl],
                            start=(k == 0), stop=(k == 7)),
                           reads=[s3, self.hT[:, k, sl]], writes=[b], signal=(k == 7))
                    t = self.tmp()
                    op("act", lambda e, t=t, a=a: e.activation(out=t, in_=a, func=AF.Silu), reads=[a], writes=[t])
                    op("dve", lambda e, t=t, b=b, j=j, sl=sl: e.tensor_tensor(
                        out=self.hid[:, j, sl], in0=t, in1=b, op=ALU.mult),
                       reads=[t, b], writes=[self.hid[:, j, sl]])
        while w2_i < len(w2_parts):
            j0, nj = w2_parts[w2_i]
            w2_i += 1
            dst = self.w2[:, j0:j0 + nj, :]
            op("pool", lambda e, dst=dst, j0=j0, nj=nj: e.dma_start(out=dst, in_=w2v[:, j0:j0 + nj, :]),
               writes=[dst], slot="w2_%d" % (w2_i - 1))
        pb = 0
        for m in range(8):
            for h in range(self.NH):
                sl = slice(h * 512, (h + 1) * 512)
                o = self.bank(4 + pb % 2)
                pb += 1
                for j in range(22):
                    op("pe", lambda e, o=o, j=j, m=m, sl=sl: e.matmul(
                        o, lhsT=self.w2[:, j, m * 128:(m + 1) * 128], rhs=self.hid[:, j, sl],
                        start=(j == 0), stop=(j == 21)),
                       reads=[self.w2[:, j, m * 128:(m + 1) * 128], self.hid[:, j, sl]], writes=[o],
                       signal=(j == 21))
                op("dve", lambda e, o=o, m=m, sl=sl: e.scalar_tensor_tensor(
                    out=self.xT[:, m, sl], in0=o, scalar=0.5, in1=self.xT[:, m, sl],
                    op0=ALU.mult, op1=ALU.add),
                   reads=[o, self.xT[:, m, sl]], writes=[self.xT[:, m, sl]])

    def proj_out(self, wname, src):
        op, w = self.op, self.w
        wv = self.wview(w[wname][0], 0, 1024)
        pb = 0
        for cb in range(2):
            s = self.wload(wv[:, :, cb * 512:(cb + 1) * 512], [8, 512])
            for mm in range(4):
                m = cb * 4 + mm
                for h in range(self.NH):
                    sl = slice(h * 512, (h + 1) * 512)
                    o = self.bank(4 + pb % 2)
                    pb += 1
                    for k in range(8):
                        op("pe", lambda e, o=o, s=s, k=k, mm=mm, sl=sl: e.matmul(
                            o, lhsT=s[:, k, mm * 128:(mm + 1) * 128], rhs=src(k, sl),
                            start=(k == 0), stop=(k == 7)),
                           reads=[s, src(k, sl)], writes=[o], signal=(k == 7))
                    op("dve", lambda e, o=o, m=m, sl=sl: e.tensor_tensor(
                        out=self.xT[:, m, sl], in0=o, in1=self.xT[:, m, sl], op=ALU.add),
                       reads=[o, self.xT[:, m, sl]], writes=[self.xT[:, m, sl]])

    def ple(self, l, ti):
        op, T, w = self.op, self.T, self.w
        self.norm("ple_norm", l * 8, self.hT)
        src = self.p_d[l, ti * T:(ti + 1) * T, :].rearrange("(g p) d -> p g d", p=128)
        op("sync", lambda e: e.dma_start(out=self.pst, in_=src), writes=[self.pst], slot="pst")
        for g in range(self.NG):
            ps = self.bank(g % 2)[:, 0:256]
            for c in range(2):
                op("pe", lambda e, ps=ps, g=g, c=c: e.transpose(
                    out=ps[:, c * 128:(c + 1) * 128], in_=self.pst[:, g, c * 128:(c + 1) * 128],
                    identity=self.ident),
                   reads=[self.pst[:, g, :], self.ident], writes=[ps], signal=(c == 1))
            dst = self.pT[:, :, g * 128:(g + 1) * 128]
            op("act", lambda e, dst=dst, ps=ps: e.copy(out=dst, in_=ps.rearrange("p (c t) -> p c t", c=2)),
               reads=[ps], writes=[dst])
        wg = self.wview(w["ple_gate"][l], 0, 1024)
        wp = self.wview(w["ple_proj"][l], 0, 1024)
        sp = self.wload(wp, [2, 1024])
        pb = 0
        for cb in range(2):
            s = self.wload(wg[:, :, cb * 512:(cb + 1) * 512], [8, 512])
            for mm in range(4):
                m = cb * 4 + mm
                for h in range(self.NH):
                    sl = slice(h * 512, (h + 1) * 512)
                    gps = self.bank(2 * (pb % 2))
                    pps = self.bank(2 * (pb % 2) + 1)
                    pb += 1
                    for k in range(8):
                        op("pe", lambda e, gps=gps, s=s, k=k, mm=mm, sl=sl: e.matmul(
                            gps, lhsT=s[:, k, mm * 128:(mm + 1) * 128], rhs=self.hT[:, k, sl],
                            start=(k == 0), stop=(k == 7)),
                           reads=[s, self.hT[:, k, sl]], writes=[gps], signal=(k == 7))
                    for c in range(2):
                        op("pe", lambda e, pps=pps, c=c, m=m, sl=sl: e.matmul(
                            pps, lhsT=sp[:, c, m * 128:(m + 1) * 128], rhs=self.pT[:, c, sl],
                            start=(c == 0), stop=(c == 1)),
                           reads=[sp, self.pT[:, c, sl]], writes=[pps], signal=(c == 1))
                    t = self.tmp()
                    op("act", lambda e, t=t, gps=gps: e.activation(out=t, in_=gps, func=AF.Sigmoid),
                       reads=[gps], writes=[t])
                    op("dve", lambda e, t=t, pps=pps: e.tensor_tensor(out=t, in0=t, in1=pps, op=ALU.mult),
                       reads=[t, pps], writes=[t])
                    op("dve", lambda e, t=t, m=m, sl=sl: e.tensor_tensor(
                        out=self.xT[:, m, sl], in0=t, in1=self.xT[:, m, sl], op=ALU.add),
                       reads=[t, self.xT[:, m, sl]], writes=[self.xT[:, m, sl]])

    def final(self, ti, do_norm=True):
        op, T = self.op, self.T
        if do_norm:
            self.norm("final_norm", 0, self.Yf)
            src = self.Yf
        else:
            src = self.xT
        dst_d = self.y_d[ti * T:(ti + 1) * T, :].rearrange("(g p) d -> p g d", p=128)
        k = 0
        for g in range(self.NG):
            for cq in range(2):
                ps = self.bank(k % 2)
                k += 1
                for c in range(4):
                    cc = cq * 4 + c
                    op("pe", lambda e, ps=ps, g=g, c=c, cc=cc: e.transpose(
                        out=ps[:, c * 128:(c + 1) * 128], in_=src[:, cc, g * 128:(g + 1) * 128],
                        identity=self.ident),
                       reads=[src[:, cc, g * 128:(g + 1) * 128], self.ident], writes=[ps], signal=(c == 3))
                dst = self.Ys[:, g, cq * 512:(cq + 1) * 512]
                if k % 2:
                    op("act", lambda e, dst=dst, ps=ps: e.copy(out=dst, in_=ps), reads=[ps], writes=[dst])
                else:
                    op("dve", lambda e, dst=dst, ps=ps: e.tensor_copy(out=dst, in_=ps), reads=[ps], writes=[dst])
            op("sync", lambda e, g=g: e.dma_start(out=dst_d[:, g, :], in_=self.Ys[:, g, :]),
               reads=[self.Ys[:, g, :]], slot="yo%d" % g)

    def build(self):
        self.declare()
        self.setup()
        self.epsc = self.view(self.alloc(4), [1], F32)
        self.op("dve", lambda e: e.memset(self.epsc, EPS), writes=[self.epsc])
        self.onec = self.view(self.alloc(4), [1], F32)
        self.op("dve", lambda e: e.memset(self.onec, 1.0), writes=[self.onec])
        self.prologue()
        stop = self.stop_after
        for ti in range(self.NT):
            self.load_x(ti)
            phases = []
            for l in range(2):
                phases += [("ffn", l, 0), ("mix", l, 0), ("ffn", l, 1), ("ple", l, 0)]
            done_all = True
            for pi, (kind, l, i) in enumerate(phases):
                if stop is not None and pi >= stop:
                    done_all = (stop == -1)
                    break
                if kind == "ffn":
                    self.ffn(l, i)
                elif kind == "mix":
                    if l == 0:
                        self.mixer0()
                    else:
                        self.mixer1()
                else:
                    self.ple(l, ti)
            self.final(ti, do_norm=done_all)
        self.S.wait_all("sync")
        self.S.emit()

    def run_streams(self, gens):
        active = list(gens)
        flags = set()
        waiting = {}
        while active:
            progressed = False
            for g in list(active):
                w = waiting.get(id(g))
                if w is not None:
                    if w not in flags:
                        continue
                    waiting[id(g)] = None
                try:
                    r = next(g)
                except StopIteration:
                    active.remove(g)
                    progressed = True
                    continue
                progressed = True
                if r is not None:
                    kind, name = r
                    if kind == "set":
                        flags.add(name)
                    elif name not in flags:
                        waiting[id(g)] = name
            assert progressed, "stream deadlock"

    def fm_proj(self, s, col0, nchunks, dst_fn, chunk0=0, banks=(0, 1)):
        op = self.op
        for ci in range(nchunks):
            for h in range(self.NH):
                sl = slice(h * 512, (h + 1) * 512)
                ps = self.bank(banks[self.pbk % len(banks)])
                self.pbk += 1
                for k in range(8):
                    op("pe", lambda e, ps=ps, k=k, ci=ci, sl=sl: e.matmul(
                        ps, lhsT=s[:, k, col0 + ci * 128:col0 + (ci + 1) * 128], rhs=self.hT[:, k, sl],
                        start=(k == 0), stop=(k == 7)),
                       reads=[s, self.hT[:, k, sl]], writes=[ps], signal=(k == 7))
                dst_fn(chunk0 + ci, h, sl, ps)
                yield

    def tm_proj(self, s, col0, ncols, tg, ps):
        op = self.op
        for k in range(8):
            op("pe", lambda e, k=k: e.matmul(
                ps, lhsT=self.hT[:, k, tg * 128:(tg + 1) * 128], rhs=s[:, k, col0:col0 + ncols],
                start=(k == 0), stop=(k == 7)),
               reads=[s, self.hT[:, k, tg * 128:(tg + 1) * 128]], writes=[ps], signal=(k == 7))

    def decay_item(self, lf, kf, ncols, tg, ngroups, kd_dst, totv, escale, ups, et):
        op = self.op
        op("pe", lambda e: e.matmul(ups, lhsT=self.U, rhs=lf, start=True, stop=True),
           reads=[lf, self.U], writes=[ups])
        for gi in range(ngroups):
            dst = totv[:, gi, 2 * tg:2 * tg + 2]
            op("pe", lambda e, gi=gi, dst=dst: e.matmul(dst, lhsT=lf[:, gi * 128:(gi + 1) * 128], rhs=self.E,
                                                 start=True, stop=True),
               reads=[lf, self.E], writes=[dst], signal=(gi == ngroups - 1))
        op("act", lambda e: e.activation(out=et, in_=ups, func=AF.Exp, scale=escale),
           reads=[ups], writes=[et])
        op("dve", lambda e: e.tensor_tensor(out=kd_dst, in0=kf, in1=et, op=ALU.mult),
           reads=[kf, et], writes=[kd_dst])

    def gla_core(self, items, hpg, kd, qT, kdec, vtm, gT, sgT, hn, sbase, banksets, rhp, pipelined, dstf):
        op, T = self.op, self.T
        sdt = self.stat_dt
        ones = self.ones if sdt == F32 else self.ones_b
        prevs = {}

        def banks_of(it):
            pb, ob, sb_ = banksets[it % len(banksets)]
            PP = self.PS[:, pb * 512:(pb + 2) * 512].rearrange("p (s c v) -> p s c v", s=2, c=4)
            return PP, [self.bank(x) for x in ob], [self.bank(x) for x in sb_]

        def kvmm(it):
            gi, hf = items[it]
            PP, oTs, sts = banks_of(it)
            for c in range(8):
                cc = hf * 8 + c
                tg, sub = cc // 2, cc % 2
                for e_ in range(hpg):
                    head = gi * hpg + e_
                    lhsT = kdec[sub * 64:(sub + 1) * 64, tg, head * kd:(head + 1) * kd]
                    rhs = vtm[sub * 64:(sub + 1) * 64, tg, head * 128:(head + 1) * 128]
                    out = PP[e_ * kd:(e_ + 1) * kd, sub, c // 2, :]
                    op("pe", lambda e, out=out, lhsT=lhsT, rhs=rhs: e.matmul(out, lhsT=lhsT, rhs=rhs,
                                                                       start=True, stop=True),
                       reads=[lhsT, rhs], writes=[out], signal=(e_ == hpg - 1))

        def chain_gen(it, omm):
            gi, hf = items[it]
            PP, oTs, sts = banks_of(it)
            prev = prevs.get(gi, self.Sst[:, sbase + gi, :])
            for c in range(8):
                cc = hf * 8 + c
                sub = cc % 2
                new = self.Str[:, self.str_i % 8, :]
                sb = self.Sb[:, self.str_i % 8, :]
                self.str_i += 1
                pin = PP[:, sub, c // 2, :]
                gcol = gT[:, gi, cc:cc + 1]
                op("dve", lambda e, new=new, prev=prev, pin=pin, gcol=gcol: e.scalar_tensor_tensor(
                    out=new, in0=prev, scalar=gcol, in1=pin, op0=ALU.mult, op1=ALU.add),
                   reads=[prev, pin, gT], writes=[new])
                op("act", lambda e, sb=sb, new=new: e.copy(out=sb, in_=new), reads=[new], writes=[sb])
                for e_ in range(hpg):
                    out = oTs[e_][:, c * 64:(c + 1) * 64]
                    lhsT = sb[e_ * kd:(e_ + 1) * kd, :]
                    rhs = qT[e_ * kd:(e_ + 1) * kd, gi, cc * 64:(cc + 1) * 64]
                    omm.append((out, lhsT, rhs, e_ == hpg - 1))
                prev = new
                yield
            prevs[gi] = prev
            if hf == self.NH - 1:
                dstS = self.Sst[:, sbase + gi, :]
                op("dve", lambda e, dstS=dstS, prev=prev: e.tensor_copy(out=dstS, in_=prev),
                   reads=[prev], writes=[dstS])

        def chain(it):
            omm = []
            for _ in chain_gen(it, omm):
                pass
            return omm

        def chain2(ita, itb):
            oa, ob = [], []
            ga, gb = chain_gen(ita, oa), chain_gen(itb, ob)
            da = db = False
            while not (da and db):
                na, nb = len(oa), len(ob)
                if not da:
                    try:
                        next(ga)
                    except StopIteration:
                        da = True
                if not db:
                    try:
                        next(gb)
                    except StopIteration:
                        db = True
                o_mm(oa[max(0, na - hpg):na] + ob[max(0, nb - hpg):nb])
            return []

        def o_mm(omm):
            for out, lhsT, rhs, sig in omm:
                op("pe", lambda e, out=out, lhsT=lhsT, rhs=rhs: e.matmul(out, lhsT=lhsT, rhs=rhs,
                                                                   start=True, stop=True),
                   reads=[lhsT, rhs], writes=[out], signal=sig)

        hstate = {}

        def h1(it):
            gi, hf = items[it]
            PP, oTs, sts = banks_of(it)
            rhs_ = []
            for e_ in range(hpg):
                head = gi * hpg + e_
                oT = oTs[e_]
                sq = self.sq[head % 2] if sdt == F32 else self.sq[head % 2].bitcast(BF16)[:, 0:512]
                op("act", lambda e, sq=sq, oT=oT: e.activation(out=sq, in_=oT, func=AF.Square),
                   reads=[oT], writes=[sq])
                st = sts[e_]
                op("pe", lambda e, st=st, sq=sq: e.matmul(st, lhsT=ones, rhs=sq, start=True, stop=True),
                   reads=[sq, ones], writes=[st])
                rsb = self.rsb[head % 2]
                op("act", lambda e, st=st, rsb=rsb: e.activation(out=rsb, in_=st, func=AF.Ln, scale=1.0 / 128,
                                                               bias=self.epsc),
                   reads=[st], writes=[rsb])
                rh = rhp[self.rh_i % len(rhp)]
                self.rh_i += 1
                op("act", lambda e, rh=rh, rsb=rsb: e.activation(out=rh, in_=rsb, func=AF.Exp, scale=-0.5),
                   reads=[rsb], writes=[rh])
                rhs_.append(rh)
            hstate[it] = rhs_

        def h2(it):
            gi, hf = items[it]
            sl = slice(hf * 512, (hf + 1) * 512)
            PP, oTs, sts = banks_of(it)
            for e_ in range(hpg):
                head = gi * hpg + e_
                oT = oTs[e_]
                rh = hstate[it][e_]
                op("dve", lambda e, rh=rh, oT=oT: e.tensor_tensor(out=rh, in0=oT, in1=rh, op=ALU.mult),
                   reads=[oT, rh], writes=[rh])
                dst = dstf(head, sl)
                op("dve", lambda e, rh=rh, dst=dst, head=head, sl=sl: e.scalar_tensor_tensor(
                    out=dst, in0=rh, scalar=hn, in1=sgT[:, head, sl], op0=ALU.mult, op1=ALU.mult),
                   reads=[rh, sgT[:, head, sl], self.vecT], writes=[dst])

        n = len(items)
        if pipelined == 2:
            assert n % 2 == 0 and len(banksets) == 2
            npair = n // 2
            kvmm(0)
            kvmm(1)
            for p in range(npair):
                a, b = 2 * p, 2 * p + 1
                if p > 0:
                    h2(a - 2)
                    h2(b - 2)
                omm = chain2(a, b)
                if p + 1 < npair:
                    kvmm(a + 2)
                    kvmm(b + 2)
                o_mm(omm)
                h1(a)
                h1(b)
                yield
            h2(n - 2)
            h2(n - 1)
            yield
        elif pipelined:
            kvmm(0)
            for it in range(n):
                omm = chain(it)
                if it > 0:
                    h2(it - 1)
                if it + 1 < n:
                    kvmm(it + 1)
                o_mm(omm)
                h1(it)
                yield
            h2(n - 1)
            yield
        else:
            for it in range(n):
                kvmm(it)
                omm = chain(it)
                yield
                yield
                o_mm(omm)
                yield
                h1(it)
                h2(it)
                yield

    def mixer1(self):
        op, T, w, NG = self.op, self.T, self.w, self.NG
        self.norm("mix_norm", 8, self.hT)
        o = self.R1
        qT = self.view(o, [8, T], BF16); o += 8 * T * 2
        kdec = self.view(o, [NG, 1024], BF16); o += NG * 2048
        vtm = self.view(o, [NG, 1024], BF16); o += NG * 2048
        sgT = self.view(o, [8, T], F32); o += 8 * T * 4
        gT = self.view(o, [8, 16], F32); o += 512
        rt = []
        while o + 2048 <= self.W2o + 22 * T * 2 and len(rt) < 3:
            rt.append(self.view(o, [512], F32)); o += 2048
        assert len(rt) == 3
        etp = rt[2:3]
        rhp = [self.rstd[:, 0:512], self.rstd[:, 512:1024]]
        tm_ = self.tmps
        fl = [(tm_[0], tm_[1]), (tm_[2], tm_[3]), (rt[0], rt[1])]
        zb = (0, 1, 3)
        self.pbk = 0
        self.str_i = 0
        self.rh_i = 0
        wv = self.wview(w["m1_w_in"][0], 0, 4096)
        totps = self.bank(7)[:, 0:128]
        totv = totps.rearrange("p (g c) -> p g c", g=8)
        gTf = gT.rearrange("p g c -> p (g c)")
        hn = self.vcol("hgrn_head_norm", 0)
        tm = self.tmps

        def P():
            k = 0
            for cb in range(2):
                s = self.wload(wv[:, :, cb * 512:(cb + 1) * 512], [8, 512])
                def evq(ci, h, sl, ps):
                    op("act", lambda e: e.activation(out=qT[:, ci, sl], in_=ps, func=AF.Silu),
                       reads=[ps], writes=[qT[:, ci, sl]])
                yield from self.fm_proj(s, 0, 4, evq, chunk0=cb * 4)
                s = self.wload(wv[:, :, 3072 + cb * 512:3072 + (cb + 1) * 512], [8, 512])
                def evg(ci, h, sl, ps):
                    op("act", lambda e: e.activation(out=sgT[:, ci, sl], in_=ps, func=AF.Silu),
                       reads=[ps], writes=[sgT[:, ci, sl]])
                yield from self.fm_proj(s, 0, 4, evg, chunk0=cb * 4)
            for cb in range(2):
                s = self.wload(wv[:, :, 1024 + cb * 512:1024 + (cb + 1) * 512], [8, 512])
                cols = slice(cb * 512, (cb + 1) * 512)
                pend = []
                for tg in range(NG):
                    ps = self.bank(zb[k % 3])
                    self.tm_proj(s, 0, 512, tg, ps)
                    f, lf = fl[k % 3]
                    op("act", lambda e, f=f, ps=ps: e.activation(out=f, in_=ps, func=AF.Exp, scale=-1.0),
                       reads=[ps], writes=[f])
                    op("dve", lambda e, f=f, lf=lf, cols=cols: e.tensor_tensor(out=lf, in0=f, in1=self.lb_b[:, cols], op=ALU.mult),
                       reads=[f, self.lb_b[:, cols]], writes=[lf])
                    op("act", lambda e, f=f: e.activation(out=f, in_=f, func=AF.Ln, bias=self.onec),
                       reads=[f], writes=[f])
                    op("act", lambda e, lf=lf: e.activation(out=lf, in_=lf, func=AF.Ln, bias=self.onec),
                       reads=[lf], writes=[lf])
                    op("dve", lambda e, f=f, lf=lf: e.tensor_tensor(out=lf, in0=lf, in1=f, op=ALU.subtract),
                       reads=[f, lf], writes=[lf])
                    op("dve", lambda e, f=f, ps=ps: e.tensor_tensor(out=f, in0=f, in1=ps, op=ALU.add),
                       reads=[f, ps], writes=[f])
                    op("act", lambda e, f=f: e.activation(out=f, in_=f, func=AF.Exp, scale=-1.0),
                       reads=[f], writes=[f])
                    op("dve", lambda e, f=f, cols=cols: e.tensor_tensor(out=f, in0=f, in1=self.oml_b[:, cols], op=ALU.mult),
                       reads=[f, self.oml_b[:, cols]], writes=[f])
                    tv = totv[:, cb * 4:(cb + 1) * 4, :]
                    pend.append((lf, f, 512, tg, 4, kdec[:, tg, cols], tv, 1.0, self.bank(2), etp[0]))
                    if len(pend) > 2:
                        self.decay_item(*pend.pop(0))
                    k += 1
                    yield
                while pend:
                    self.decay_item(*pend.pop(0))
                op("act", lambda e, cb=cb: e.activation(out=gTf[:, cb * 64:(cb + 1) * 64],
                                                        in_=totps[:, cb * 64:(cb + 1) * 64], func=AF.Exp),
                   reads=[totps], writes=[gT])
                s = self.wload(wv[:, :, 2048 + cb * 512:2048 + (cb + 1) * 512], [8, 512])
                for tg in range(NG):
                    ps = self.bank(zb[tg % 3])
                    self.tm_proj(s, 0, 512, tg, ps)
                    dst = vtm[:, tg, cols]
                    if tg % 2:
                        op("act", lambda e, dst=dst, ps=ps: e.copy(out=dst, in_=ps), reads=[ps], writes=[dst])
                    else:
                        op("dve", lambda e, dst=dst, ps=ps: e.tensor_copy(out=dst, in_=ps), reads=[ps], writes=[dst])
                    yield
                yield ("set", "cb%d" % cb)

        def dstf(head, sl):
            return qT[:, head, sl]

        def C():
            yield ("wait", "cb0")
            setA = (4, [6], [4])
            itemsA = [(gi, hf) for gi in range(0, 4) for hf in range(self.NH)]
            yield from self.gla_core(itemsA, 1, 128, qT, kdec, vtm, gT, sgT, hn, 2, [setA], rhp, False, dstf)
            yield ("wait", "cb1")
            setA2 = (4, [6], [7])
            setB2 = (0, [2], [3])
            itemsB = []
            for g0 in (4, 6):
                for hf in range(self.NH):
                    itemsB += [(g0, hf), (g0 + 1, hf)]
            yield from self.gla_core(itemsB, 1, 128, qT, kdec, vtm, gT, sgT, hn, 2, [setA2, setB2],
                                     rhp + list(self.tmps[0:2]), 2, dstf)

        self.run_streams([P(), C()])
        self.proj_out("m1_w_out", lambda k, sl: qT[:, k, sl])

    def mixer0(self):
        op, T, w, NG = self.op, self.T, self.w, self.NG
        self.norm("mix_norm", 0, self.hT)
        o = self.R1
        qT = self.view(o, [2, T], BF16); o += 2 * T * 2
        lrT = self.view(o, [T], BF16); o += T * 2
        kdec = self.view(o, [NG, 256], BF16); o += NG * 512
        vtm = self.view(o, [NG, 512], BF16); o += NG * 1024
        sgT = self.view(o, [4, T], F32); o += 4 * T * 4
        ggT = self.view(o, [4, T], F32); o += 4 * T * 4
        xrT = self.view(o, [4, T + 4], F32)
        yb = [self.view(o + c * (T + 4) * 4, [T], BF16) for c in range(4)]
        o += 4 * (T + 4) * 4
        gT = self.view(o, [2, 16], F32); o += 128
        xc = self.view(o, [T], F32); o += T * 4
        xcb = self.view(o, [T], BF16); o += T * 2
        lt = []
        for _ in range(6):
            lt.append(self.view(o, [512], F32)); o += 2048
        rhp = [self.rstd[:, 0:512], self.rstd[:, 512:1024]]
        assert o <= self.W2o + 22 * T * 2, o
        self.pbk = 0
        self.str_i = 0
        self.rh_i = 0
        wv = self.wview(w["m0_w_in"][0], 0, M0C)
        tm = self.tmps
        totps = self.bank(3)[:, 0:32]
        totv = totps.rearrange("p (g c) -> p g c", g=2)
        hn = self.vcol("gla_head_norm", 0)

        def P():
            sB4 = self.wload(wv[:, :, 1552:2064], [8, 512])
            for c in range(4):
                op("dve", lambda e, c=c: e.tensor_copy(out=xrT[:, c, 0:3], in_=self.xr_carry[:, c, 0:3]),
                   reads=[self.xr_carry], writes=[xrT[:, c, 0:3]])
            def evxr(ci, h, sl, ps):
                dst = xrT[:, ci, 3 + h * 512:3 + (h + 1) * 512]
                op("act", lambda e: e.copy(out=dst, in_=ps), reads=[ps], writes=[dst])
            yield from self.fm_proj(sB4, 0, 4, evxr)
            yield ("set", "xr")
            sB5 = self.wload(wv[:, :, 2064:2576], [8, 512])
            def evxg(ci, h, sl, ps):
                t = tm[3]
                op("act", lambda e: e.activation(out=t, in_=ps, func=AF.Square), reads=[ps], writes=[t])
                op("dve", lambda e: e.tensor_scalar(out=t, in0=t, scalar1=0.044715, scalar2=1.0, op0=ALU.mult, op1=ALU.add),
                   reads=[t], writes=[t])
                op("dve", lambda e: e.tensor_tensor(out=t, in0=t, in1=ps, op=ALU.mult), reads=[t, ps], writes=[t])
                op("act", lambda e: e.activation(out=t, in_=t, func=AF.Sigmoid, scale=1.5957691216057308),
                   reads=[t], writes=[t])
                op("dve", lambda e: e.tensor_tensor(out=ggT[:, ci, sl], in0=t, in1=ps, op=ALU.mult),
                   reads=[t, ps], writes=[ggT[:, ci, sl]])
            yield from self.fm_proj(sB5, 0, 4, evxg)
            yield ("set", "gg")
            sB0 = self.wload(wv[:, :, 0:512], [8, 512])
            sLR = self.wload(wv[:, :, 1536:1552], [8, 16])
            sB1 = self.wload(wv[:, :, 512:1024], [8, 512])
            def evq(ci, h, sl, ps):
                op("act", lambda e: e.mul(out=qT[:, ci, sl], in_=ps, mul=0.125), reads=[ps], writes=[qT[:, ci, sl]])
            yield from self.fm_proj(sB0, 0, 2, evq)
            for h in range(self.NH):
                sl = slice(h * 512, (h + 1) * 512)
                ps = self.bank(h % 2)
                for k in range(8):
                    op("pe", lambda e, ps=ps, k=k, sl=sl: e.matmul(ps[0:16, :], lhsT=sLR[:, k, 0:16], rhs=self.hT[:, k, sl],
                                                             start=(k == 0), stop=(k == 7)),
                       reads=[sLR, self.hT[:, k, sl]], writes=[ps], signal=(k == 7))
                op("act", lambda e, ps=ps, sl=sl: e.copy(out=lrT[0:16, sl], in_=ps[0:16, :]), reads=[ps], writes=[lrT[0:16, sl]])
            yield
            pend = None
            for tg in range(NG):
                bk = self.bank(tg % 2)
                lps = bk[:, 0:256]
                kps = self.bank(4)[:, 0:256]
                op("pe", lambda e, lps=lps, tg=tg: e.matmul(lps, lhsT=lrT[0:16, tg * 128:(tg + 1) * 128],
                                                      rhs=self.gup[0:16, :], start=True, stop=True),
                   reads=[lrT[0:16, tg * 128:(tg + 1) * 128], self.gup], writes=[lps])
                sp = tm[tg % 2][:, 0:256]
                op("dve", lambda e, sp=sp, lps=lps: e.tensor_tensor(out=sp, in0=lps, in1=self.gbias_b, op=ALU.add),
                   reads=[lps, self.gbias_b], writes=[sp])
                op("act", lambda e, sp=sp: e.activation(out=sp, in_=sp, func=AF.Exp, scale=-1.0), reads=[sp], writes=[sp])
                op("act", lambda e, sp=sp: e.activation(out=sp, in_=sp, func=AF.Ln, bias=self.onec), reads=[sp], writes=[sp])
                kf = tm[tg % 2][:, 256:512]
                self.tm_proj(sB0, 256, 256, tg, kps)
                op("act", lambda e, kf=kf, kps=kps: e.copy(out=kf, in_=kps), reads=[kps], writes=[kf])
                vps = self.bank(5)
                self.tm_proj(sB1, 0, 512, tg, vps)
                dst = vtm[:, tg, :]
                op("dve", lambda e, dst=dst, vps=vps: e.tensor_copy(out=dst, in_=vps), reads=[vps], writes=[dst])
                if pend is not None:
                    self.decay_item(*pend)
                pend = (sp, kf, 256, tg, 2, kdec[:, tg, :], totv, -1.0 / 16.0, self.bank(2)[:, 0:256], tm[2][:, 0:256])
                yield
            self.decay_item(*pend)
            op("act", lambda e: e.activation(out=gT.rearrange("p g c -> p (g c)"), in_=totps, func=AF.Exp,
                                             scale=-1.0 / 16.0), reads=[totps], writes=[gT])
            sB2 = self.wload(wv[:, :, 1024:1536], [8, 512])
            def evg(ci, h, sl, ps):
                op("act", lambda e: e.activation(out=sgT[:, ci, sl], in_=ps, func=AF.Silu), reads=[ps], writes=[sgT[:, ci, sl]])
            yield from self.fm_proj(sB2, 0, 4, evg)
            yield ("set", "kv")

        def L():
            yield ("wait", "xr")
            yield ("wait", "gg")
            its = [(c, h) for c in range(4) for h in range(self.NH)]

            def conv(c):
                w0 = self.vcol("lru_conv_w", 0 * 4 + c)
                cb_ = self.vcol("lru_conv_b", c)
                op("dve", lambda e: e.tensor_scalar(out=xc, in0=xrT[:, c, 0:T], scalar1=w0, scalar2=cb_,
                                                    op0=ALU.mult, op1=ALU.add),
                   reads=[xrT[:, c, :], self.vecT], writes=[xc])
                for j in range(1, 4):
                    wj = self.vcol("lru_conv_w", j * 4 + c)
                    op("dve", lambda e, j=j, wj=wj: e.scalar_tensor_tensor(
                        out=xc, in0=xrT[:, c, j:j + T], scalar=wj, in1=xc, op0=ALU.mult, op1=ALU.add),
                       reads=[xrT[:, c, :], xc, self.vecT], writes=[xc])
                op("dve", lambda e: e.tensor_copy(out=self.xr_carry[:, c, 0:3], in_=xrT[:, c, T:T + 3]),
                   reads=[xrT[:, c, :]], writes=[self.xr_carry])
                op("act", lambda e: e.copy(out=xcb, in_=xc), reads=[xc], writes=[xcb])

            def stageA(k):
                c, h = its[k]
                sl = slice(h * 512, (h + 1) * 512)
                a_, e2, u_ = lt[3 * (k % 2)], lt[3 * (k % 2) + 1], lt[3 * (k % 2) + 2]
                ra = self.bank(6)
                ia = self.bank(7)
                ba = self.vcol("lru_ba", c)
                bx = self.vcol("lru_bx", c)
                op("pe", lambda e: e.matmul(ra, lhsT=self.bd[:, 0, c, :], rhs=xcb[:, sl], start=True, stop=True),
                   reads=[self.bd, xcb[:, sl]], writes=[ra])
                op("pe", lambda e: e.matmul(ia, lhsT=self.bd[:, 1, c, :], rhs=xcb[:, sl], start=True, stop=True),
                   reads=[self.bd, xcb[:, sl]], writes=[ia])
                op("act", lambda e: e.activation(out=ra, in_=ra, func=AF.Sigmoid, bias=ba),
                   reads=[ra, self.vecT], writes=[ra])
                op("act", lambda e: e.activation(out=u_, in_=ia, func=AF.Sigmoid, bias=bx),
                   reads=[ia, self.vecT], writes=[u_])
                op("act", lambda e: e.activation(out=a_, in_=ra, func=AF.Exp, scale=self.clam[:, c:c + 1]),
                   reads=[ra, self.clam], writes=[a_])
                op("act", lambda e: e.activation(out=e2, in_=ra, func=AF.Exp, scale=self.clam[:, 4 + c:5 + c]),
                   reads=[ra, self.clam], writes=[e2])
                op("dve", lambda e: e.tensor_tensor(out=u_, in0=u_, in1=xc[:, sl], op=ALU.mult),
                   reads=[u_, xc[:, sl]], writes=[u_])

            def stageB(k):
                c, h = its[k]
                sl = slice(h * 512, (h + 1) * 512)
                a_, e2, u_ = lt[3 * (k % 2)], lt[3 * (k % 2) + 1], lt[3 * (k % 2) + 2]
                op("dve", lambda e: e.tensor_scalar(out=e2, in0=e2, scalar1=1.0 - 1e-6, scalar2=-1.0, op0=ALU.min, op1=ALU.mult),
                   reads=[e2], writes=[e2])
                op("act", lambda e: e.activation(out=e2, in_=e2, func=AF.Ln, bias=self.onec), reads=[e2], writes=[e2])
                op("act", lambda e: e.activation(out=e2, in_=e2, func=AF.Exp, scale=0.5), reads=[e2], writes=[e2])
                op("dve", lambda e: e.tensor_tensor(out=u_, in0=u_, in1=e2, op=ALU.mult), reads=[u_, e2], writes=[u_])
                op("dve", lambda e: e.tensor_tensor_scan(out=e2, data0=a_, data1=u_, initial=self.hprev[:, c:c + 1],
                                                        op0=ALU.mult, op1=ALU.add),
                   reads=[a_, u_, self.hprev], writes=[e2])
                op("dve", lambda e: e.tensor_copy(out=self.hprev[:, c:c + 1], in_=e2[:, 511:512]),
                   reads=[e2], writes=[self.hprev])
                dst = yb[c][:, sl]
                op("dve", lambda e: e.tensor_tensor(out=dst, in0=e2, in1=ggT[:, c, sl], op=ALU.mult),
                   reads=[e2, ggT[:, c, sl]], writes=[dst])

            for k in range(len(its)):
                if its[k][1] == 0:
                    conv(its[k][0])
                    yield
                stageA(k)
                yield
                if k > 0:
                    stageB(k - 1)
                    yield
            stageB(len(its) - 1)

        def C():
            yield ("wait", "kv")
            items = [(gi, hf) for gi in range(2) for hf in range(self.NH)]
            setA = (0, [2, 3], [0, 1])
            yield from self.gla_core(items, 2, 64, qT, kdec, vtm, gT, sgT, hn, 0, [setA], rhp, False,
                                     lambda head, sl: self.hT[:, head, sl])

        self.run_streams([P(), L(), C()])
        self.proj_out("m0_w_out", lambda k, sl: (self.hT[:, k, sl] if k < 4 else yb[k - 4][:, sl]))


def build_nc(T=1024, NT=4, stop_after=None, stat_dt=BF16):
    nc = bass.Bass("TRN2", target_bir_lowering=False)
    with ExitStack() as st:
        b = Builder(nc, st, T=T, NT=NT, stop_after=stop_after, stat_dt=stat_dt)
        b.build()
        info = (b.S.nops, b.S.nwaits, b.aoff)
    return nc, info


def kernel(**inputs):
    n = 8
    nc, info = build_nc()
    consts = make_consts()
    x = np.ascontiguousarray(inputs["x"], dtype=np.float32)
    p = np.ascontiguousarray(inputs["p"], dtype=np.float32)
    wts = {k: np.ascontiguousarray(inputs[k], dtype=np.float32) for k in W_SHAPES}
    in_maps = []
    for c in range(n):
        m = {"x": x[c], "p": np.ascontiguousarray(p[:, c]), "consts": consts}
        m.update(wts)
        in_maps.append(m)
    res = run_bass_kernel_spmd(nc, in_maps, core_ids=list(range(n)))
    return np.stack([r["y"] for r in res.results], axis=0)
```

```python
import numpy as np
from contextlib import ExitStack
import concourse.bass as bass
import concourse.mybir as mybir
from concourse.bass_utils import run_bass_kernel_spmd

F32 = mybir.dt.float32
BF16 = mybir.dt.bfloat16
AF = mybir.ActivationFunctionType
ALU = mybir.AluOpType

ENGS = ("sync", "act", "pe", "dve", "pool")
D = 1024
SEQ = 4096
DFF = 2816
EPS = 1e-6
M0C = 2576


def _esz(dt):
    return 4 if dt == F32 else 2


class Sched:
    def __init__(self, nc, stack):
        self.nc = nc
        self.stack = stack
        self.streams = {e: [] for e in ENGS}
        self.sem = {}
        self.cnt = {}
        self.seen = {e: {} for e in ENGS}
        self.snap = {}
        self.last_w = {}
        self.readers = {}
        self.pending = {e: ([], []) for e in ENGS}
        self.nwaits = 0
        self.nops = 0

    def _sem(self, v):
        if v not in self.sem:
            self.sem[v] = self.stack.enter_context(self.nc.semaphore("s_" + v))
            self.cnt[v] = 0
        return self.sem[v]

    @staticmethod
    def keys(ap):
        if isinstance(ap, (str, tuple)):
            return [ap]
        name = ap.tensor.name
        pat = ap.ap
        pstride = pat[0][0]
        esz = _esz(ap.dtype)
        off = ap.offset
        p0 = off // pstride
        col = off % pstride
        ext = 1
        for st, c in pat[1:]:
            ext += (c - 1) * abs(st)
        gran = 2048 if name.startswith("PS") else 256
        b0 = (col * esz) // gran
        b1 = ((col + ext) * esz - 1) // gran
        p1 = p0 + pat[0][1] - 1
        halves = set()
        if p0 < 64:
            halves.add(0)
        if p1 >= 64:
            halves.add(1)
        return [(name, b, h) for b in range(b0, b1 + 1) for h in halves]

    def op(self, eng, fn, reads=(), writes=(), slot=None, signal=True):
        rk = [k for a in reads for k in self.keys(a)]
        wk = [k for a in writes for k in self.keys(a)]
        need = {}

        def req(tok):
            if tok is not None and need.get(tok[0], 0) < tok[1]:
                need[tok[0]] = tok[1]

        for k in rk:
            req(self.last_w.get(k))
        for k in wk:
            req(self.last_w.get(k))
            r = self.readers.get(k)
            if r:
                for v, c in r.items():
                    req((v, c))
        if slot is not None:
            pv = self.cnt.get("d_" + slot, 0)
            if pv > 0:
                req(("d_" + slot, pv))
        seen = self.seen[eng]
        waits = []
        for v, c in need.items():
            if c > self.cnt.get(v, 0):
                if v == eng:
                    continue
                print("WARNING future-wait", eng, "on", v, c, self.cnt.get(v, 0))
            if seen.get(v, 0) >= c:
                continue
            waits.append((v, c))
            seen[v] = c
            sn = self.snap.get((v, c))
            if sn:
                for v2, c2 in sn.items():
                    if seen.get(v2, 0) < c2:
                        seen[v2] = c2
        self.nwaits += len(waits)
        self.nops += 1
        veng = eng if slot is None else "d_" + slot
        inc = 1 if slot is None else 16
        self._sem(veng)
        tok = (veng, self.cnt[veng] + inc)
        for k in rk:
            self.readers.setdefault(k, {})[veng] = tok[1]
        for k in wk:
            self.last_w[k] = tok
            self.readers[k] = {}
        if not signal:
            assert slot is None
            self.streams[eng].append((waits, fn, None, 0))
            return
        self.cnt[veng] += inc
        self.snap[tok] = dict(seen)
        self.streams[eng].append((waits, fn, veng, inc))

    def wait_all(self, eng):
        waits = []
        for v, c in self.cnt.items():
            if c > 0 and self.seen[eng].get(v, 0) < c:
                waits.append((v, c))
                self.seen[eng][v] = c
        self.streams[eng].append((waits, None, None, 0))

    def emit(self):
        nc = self.nc
        with nc.Block() as block:
            def run(name, e):
                for waits, fn, veng, inc in self.streams[name]:
                    for v, c in waits:
                        e.wait_ge(self.sem[v], c)
                    if fn is None:
                        continue
                    ins = fn(e)
                    if veng is not None:
                        ins.then_inc(self.sem[veng], inc)

            @block.sync
            def _(e):
                run("sync", e)

            @block.scalar
            def _(e):
                run("act", e)

            @block.tensor
            def _(e):
                run("pe", e)

            @block.vector
            def _(e):
                run("dve", e)

            @block.gpsimd
            def _(e):
                run("pool", e)


VROWS = {}
_r = 0
for _n, _k in (("ffn_norm", 32), ("mix_norm", 16), ("ple_norm", 16), ("final_norm", 8),
               ("gla_head_norm", 1), ("hgrn_head_norm", 1), ("lru_conv_w", 16),
               ("lru_conv_b", 4), ("lru_ba", 4), ("lru_bx", 4), ("lru_lambda", 4)):
    VROWS[_n] = (_r, _k)
    _r += _k
NVROWS = _r

W_SHAPES = {
    "ffn_norm": [2, 2, 1024], "ffn_w1": [2, 2, 1024, 2816], "ffn_w3": [2, 2, 1024, 2816],
    "ffn_w2": [2, 2, 2816, 1024], "mix_norm": [2, 1024], "ple_norm": [2, 1024],
    "ple_proj": [2, 256, 1024], "ple_gate": [2, 1024, 1024], "final_norm": [1024],
    "m0_w_in": [1, 1024, 2576], "gla_gate_up": [1, 16, 256], "gla_gate_bias": [1, 256],
    "gla_head_norm": [1, 128], "lru_conv_w": [1, 4, 512], "lru_conv_b": [1, 512],
    "lru_wa": [1, 8, 64, 64], "lru_ba": [1, 512], "lru_wx": [1, 8, 64, 64], "lru_bx": [1, 512],
    "lru_lambda": [1, 512], "m0_w_out": [1, 1024, 1024], "m1_w_in": [1, 1024, 4096],
    "hgrn_lb_logits": [2, 1024], "hgrn_head_norm": [1, 128], "m1_w_out": [1, 1024, 1024],
}


def make_consts():
    c = np.zeros((128, 3 * 128 + 2), np.float32)
    c[:, 0:128] = np.eye(128, dtype=np.float32)
    c[:, 128:256] = 1.0
    s = np.arange(128)[:, None]
    t = np.arange(128)[None, :]
    c[:, 256:384] = ((s > t) & ((s // 64) == (t // 64))).astype(np.float32)
    c[:, 384] = (np.arange(128) < 64)
    c[:, 385] = (np.arange(128) >= 64)
    return c


class Builder:
    def __init__(self, nc, stack, T=1024, NT=4, stop_after=None, stat_dt=F32):
        self.nc = nc
        self.st = stack
        self.T = T
        self.NT = NT
        self.NH = T // 512
        self.NG = T // 128
        self.stop_after = stop_after
        self.stat_dt = stat_dt
        self.S = Sched(nc, stack)
        self.ws_i = 0
        self.norm_done = None
        self.next_norm = None
        self.tmp_i = 0
        self.ps_rot = {}

    def view(self, off, shape, dt):
        esz = _esz(dt)
        n = int(np.prod(shape))
        assert off % 4 == 0
        a = self.arena[:, off // 2: off // 2 + n * esz // 2]
        if dt == F32:
            a = a.bitcast(F32)
        if len(shape) == 2:
            a = a.rearrange("p (a b) -> p a b", a=shape[0])
        elif len(shape) == 3:
            a = a.rearrange("p (a b c) -> p a b c", a=shape[0], b=shape[1])
        return a

    def alloc(self, nbytes):
        if nbytes >= 256:
            self.aoff = (self.aoff + 255) // 256 * 256
        off = self.aoff
        self.aoff += (nbytes + 3) // 4 * 4
        return off

    def bank(self, i):
        return self.PS[:, i * 512:(i + 1) * 512]

    def op(self, *a, **k):
        self.S.op(*a, **k)

    def declare(self):
        nc = self.nc
        NTOK = self.T * self.NT
        self.x_d = nc.dram_tensor("x", [NTOK, D], F32, kind="ExternalInput").ap()
        self.p_d = nc.dram_tensor("p", [2, NTOK, 256], F32, kind="ExternalInput").ap()
        self.w = {}
        for n, shp in W_SHAPES.items():
            self.w[n] = nc.dram_tensor(n, shp, F32, kind="ExternalInput").ap()
        self.c_d = nc.dram_tensor("consts", [128, 386], F32, kind="ExternalInput").ap()
        self.y_d = nc.dram_tensor("y", [NTOK, D], F32, kind="ExternalOutput").ap()

    def setup(self):
        nc, st, T = self.nc, self.st, self.T
        ARENA = 212480
        self.arena = st.enter_context(nc.sbuf_tensor("arena", [128, ARENA // 2], BF16))
        self.PS = st.enter_context(nc.psum_tensor("PS", [128, 4096], F32))
        self.aoff = 0
        A = self.alloc
        self.xT = self.view(A(8 * T * 4), [8, T], F32)
        self.hT = self.view(A(8 * T * 2), [8, T], BF16)
        self.R1 = A(22 * T * 2)
        self.W2o = A(22 * T * 2)
        self.RING = 24576
        self.WSbase = A(self.RING)
        self.ring_off = 0
        self.sq = [self.view(A(2048), [512], F32) for _ in range(2)]
        self.rs = self.view(A(2048), [512], F32)
        self.rsb = [self.rs, self.view(A(2048), [512], F32)]
        self.rstd = self.view(A(T * 4), [T], F32)
        self.tmps = [self.view(A(2048), [512], F32) for _ in range(4)]
        self.cst = self.view(A(386 * 4), [386], F32)
        self.ident = self.cst[:, 0:128]
        self.ones = self.cst[:, 128:256]
        self.U = self.cst[:, 256:384]
        self.E = self.cst[:, 384:386]
        self.ones_b = self.view(A(256), [128], BF16)
        self.vstage = self.view(A(512), [128], F32)
        self.vecT = self.view(A(128 * 4), [128], F32)
        self.gbias_b = self.view(A(1024), [256], F32)
        self.lb_b = self.view(A(4096), [1024], F32)
        self.oml_b = self.view(A(4096), [1024], F32)
        self.gup = self.view(A(512), [256], BF16)
        self.bdf = self.view(self.R1, [2, 4, 128], F32)
        self.bd = self.view(A(2048), [2, 4, 128], BF16)
        self.clam = self.view(A(32), [8], F32)
        self.Sst = self.view(A(10 * 512), [10, 128], F32)
        self.Str = self.view(A(8 * 512), [8, 128], F32)
        self.Sb = self.view(A(8 * 256), [8, 128], BF16)
        self.hprev = self.view(A(16), [4], F32)
        self.aoff_end = self.aoff
        assert self.aoff <= ARENA, self.aoff
        R1, W2o = self.R1, self.W2o
        self.hid = self.view(R1, [22, T], BF16)
        self.w2 = self.view(W2o, [22, 1024], BF16)
        self.Xs = self.view(R1, [self.NG, 1024], F32)
        self.Yf = self.view(R1, [8, T], F32)
        self.Ys = self.view(W2o, [self.NG, 1024], F32)
        self.pst = self.view(W2o, [self.NG, 256], F32)
        self.pT = self.view(W2o + self.NG * 1024, [2, T], BF16)
        self.xr_carry = self.view(A(64), [4, 4], F32)

    def tmp(self):
        t = self.tmps[self.tmp_i % 4]
        self.tmp_i += 1
        return t

    def wload(self, src, shape):
        nbytes = int(np.prod(shape)) * 2
        if self.ring_off + nbytes > self.RING:
            self.ring_off = 0
        dst = self.view(self.WSbase + self.ring_off, shape, BF16)
        self.ring_off += nbytes
        name = "ws%d" % (self.ws_i % 8)
        self.ws_i += 1
        self.op("pool", lambda e, dst=dst, src=src: e.dma_start(out=dst, in_=src),
                writes=[dst], slot=name)
        return dst

    def wview(self, wap, c0, c1):
        return wap.rearrange("(k p) n -> p k n", p=128)[:, :, c0:c1]

    def prologue(self):
        op, w = self.op, self.w
        op("sync", lambda e: e.dma_start(out=self.cst, in_=self.c_d), writes=[self.cst], slot="c0")
        op("dve", lambda e: e.memset(self.vstage, 0.0), writes=[self.vstage])
        for i, (n, (r0, k)) in enumerate(VROWS.items()):
            src = w[n]
            shp = W_SHAPES[n]
            if len(shp) == 3:
                src = src.rearrange("a b (c p) -> (a b c) p", p=128)
            elif len(shp) == 2:
                src = src.rearrange("a (c p) -> (a c) p", p=128)
            else:
                src = src.rearrange("(c p) -> c p", p=128)
            op("sync", lambda e, src=src, r0=r0, k=k: e.dma_start(out=self.vstage[r0:r0 + k, :], in_=src),
               writes=[self.vstage], slot="v%d" % i)
        ps = self.bank(0)[:, 0:128]
        op("pe", lambda e: e.transpose(out=ps, in_=self.vstage, identity=self.ident),
           reads=[self.vstage, self.ident], writes=[ps])
        op("dve", lambda e: e.tensor_copy(out=self.vecT, in_=ps), reads=[ps], writes=[self.vecT])
        op("dve", lambda e: e.tensor_copy(out=self.ones_b, in_=self.ones), reads=[self.ones], writes=[self.ones_b])
        op("sync", lambda e: e.dma_start(out=self.gbias_b, in_=w["gla_gate_bias"][0].partition_broadcast(128)),
           writes=[self.gbias_b], slot="b0")
        t01 = self.view(self.W2o, [2, 1024], F32)
        op("sync", lambda e: e.dma_start(out=t01[:, 0, :], in_=w["hgrn_lb_logits"][0].partition_broadcast(128)),
           writes=[t01[:, 0, :]], slot="b1")
        op("sync", lambda e: e.dma_start(out=t01[:, 1, :], in_=w["hgrn_lb_logits"][1].partition_broadcast(128)),
           writes=[t01[:, 1, :]], slot="b2")
        op("dve", lambda e: e.tensor_tensor(out=t01[:, 0, :], in0=t01[:, 1, :], in1=t01[:, 0, :], op=ALU.subtract),
           reads=[t01], writes=[t01[:, 0, :]])
        op("act", lambda e: e.activation(out=self.lb_b, in_=t01[:, 0, :], func=AF.Sigmoid),
           reads=[t01[:, 0, :]], writes=[self.lb_b])
        op("dve", lambda e: e.tensor_scalar(out=self.oml_b, in0=self.lb_b, scalar1=-1.0, scalar2=1.0,
                                            op0=ALU.mult, op1=ALU.add),
           reads=[self.lb_b], writes=[self.oml_b])
        gst = self.tmps[1]
        op("sync", lambda e: e.dma_start(out=gst[0:16, 0:256], in_=w["gla_gate_up"][0]),
           writes=[gst], slot="b3")
        op("dve", lambda e: e.tensor_copy(out=self.gup[0:16, :], in_=gst[0:16, 0:256]), reads=[gst], writes=[self.gup])
        op("dve", lambda e: e.memset(self.bdf, 0.0), writes=[self.bdf])
        for wi, n in enumerate(("lru_wa", "lru_wx")):
            for g in range(8):
                h = g % 2
                dst = self.bdf[h * 64:(h + 1) * 64, wi, g // 2, h * 64:(h + 1) * 64]
                op("sync", lambda e, dst=dst, n=n, g=g: e.dma_start(out=dst, in_=w[n][0, g]),
                   writes=[self.bdf], slot="bd%d_%d" % (wi, g))
        op("dve", lambda e: e.tensor_copy(out=self.bd, in_=self.bdf), reads=[self.bdf], writes=[self.bd])
        r0 = VROWS["lru_lambda"][0]
        lam = self.vecT[:, r0:r0 + 4]
        t = self.tmps[2][:, 0:4]
        op("act", lambda e: e.activation(out=t, in_=lam, func=AF.Exp, scale=-1.0), reads=[self.vecT], writes=[t])
        op("act", lambda e: e.activation(out=t, in_=t, func=AF.Ln, bias=self.onec), reads=[t], writes=[t])
        op("dve", lambda e: e.tensor_scalar(out=self.clam[:, 0:4], in0=t, scalar1=-8.0, scalar2=None, op0=ALU.mult),
           reads=[t], writes=[self.clam])
        op("dve", lambda e: e.tensor_scalar(out=self.clam[:, 4:8], in0=t, scalar1=-16.0, scalar2=None, op0=ALU.mult),
           reads=[t], writes=[self.clam])
        op("dve", lambda e: e.memset(self.Sst, 0.0), writes=[self.Sst])
        op("dve", lambda e: e.memset(self.hprev, 0.0), writes=[self.hprev])
        op("dve", lambda e: e.memset(self.xr_carry, 0.0), writes=[self.xr_carry])

    def vcol(self, name, idx):
        r0, k = VROWS[name]
        assert idx < k
        return self.vecT[:, r0 + idx:r0 + idx + 1]

    def load_x(self, ti):
        op, T = self.op, self.T
        src = self.x_d[ti * T:(ti + 1) * T, :].rearrange("(g p) d -> p g d", p=128)
        for g in range(self.NG):
            op("sync", lambda e, g=g: e.dma_start(out=self.Xs[:, g, :], in_=src[:, g, :]),
               writes=[self.Xs[:, g, :]], slot="xs%d" % g)
        k = 0
        for g in range(self.NG):
            for cq in range(2):
                ps = self.bank(k % 2)
                k += 1
                for c in range(4):
                    cc = cq * 4 + c
                    op("pe", lambda e, ps=ps, g=g, c=c, cc=cc: e.transpose(
                        out=ps[:, c * 128:(c + 1) * 128], in_=self.Xs[:, g, cc * 128:(cc + 1) * 128],
                        identity=self.ident),
                       reads=[self.Xs[:, g, cc * 128:(cc + 1) * 128], self.ident], writes=[ps], signal=(c == 3))
                dst = self.xT[:, cq * 4:(cq + 1) * 4, g * 128:(g + 1) * 128]
                eng = "act" if (k % 2) else "dve"
                if eng == "act":
                    op("act", lambda e, dst=dst, ps=ps: e.copy(out=dst, in_=ps.rearrange("p (c t) -> p c t", c=4)),
                       reads=[ps], writes=[dst])
                else:
                    op("dve", lambda e, dst=dst, ps=ps: e.tensor_copy(out=dst, in_=ps.rearrange("p (c t) -> p c t", c=4)),
                       reads=[ps], writes=[dst])

    def norm_half(self, h, gname, gidx0, dst, src=None, stat_bank=7):
        op, T = self.op, self.T
        src = self.xT if src is None else src
        sdt = self.stat_dt
        ones = self.ones if sdt == F32 else self.ones_b
        sl = slice(h * 512, (h + 1) * 512)
        ps = self.bank(stat_bank)
        for c in range(8):
            sq = self.sq[c % 2] if sdt == F32 else self.sq[c % 2].bitcast(BF16)[:, 0:512]
            op("act", lambda e, sq=sq, c=c, sl=sl: e.activation(out=sq, in_=src[:, c, sl], func=AF.Square),
               reads=[src[:, c, sl]], writes=[sq])
            op("pe", lambda e, sq=sq, c=c, ps=ps: e.matmul(ps, lhsT=ones, rhs=sq, start=(c == 0), stop=(c == 7)),
               reads=[sq, ones], writes=[ps])
        op("act", lambda e, ps=ps: e.activation(out=self.rs, in_=ps, func=AF.Ln, scale=1.0 / D, bias=self.epsc),
           reads=[ps], writes=[self.rs])
        op("act", lambda e, sl=sl: e.activation(out=self.rstd[:, sl], in_=self.rs, func=AF.Exp, scale=-0.5),
           reads=[self.rs], writes=[self.rstd[:, sl]])
        for c in range(8):
            g = self.vcol(gname, gidx0 + c)
            op("dve", lambda e, c=c, sl=sl, g=g: e.scalar_tensor_tensor(
                out=dst[:, c, sl], in0=src[:, c, sl], scalar=g, in1=self.rstd[:, sl],
                op0=ALU.mult, op1=ALU.mult),
               reads=[src[:, c, sl], self.rstd[:, sl], self.vecT], writes=[dst[:, c, sl]])

    def norm(self, gname, gidx0, dst):
        if self.norm_done == (gname, gidx0):
            self.norm_done = None
            return
        for h in range(self.NH):
            self.norm_half(h, gname, gidx0, dst)

    def tail_norm(self, h):
        if self.next_norm is None:
            return
        gname, gidx0, dst = self.next_norm
        self.norm_half(h, gname, gidx0, dst)
        if h == self.NH - 1:
            self.norm_done = (gname, gidx0)

    def ffn(self, l, i):
        op, T, w = self.op, self.T, self.w
        self.norm("ffn_norm", (l * 2 + i) * 8, self.hT)
        w1 = self.wview(w["ffn_w1"][l, i], 0, DFF)
        w3 = self.wview(w["ffn_w3"][l, i], 0, DFF)
        w2v = w["ffn_w2"][l, i].rearrange("(j p) n -> p j n", p=128)
        blocks = [(b * 256, 256) for b in range(11)]
        w2_parts = [(j, min(2, 22 - j)) for j in range(0, 22, 2)]
        w2_i = 0
        pa = 0
        for bi, (c0, nc_) in enumerate(blocks):
            s1 = self.wload(w1[:, :, c0:c0 + nc_], [8, nc_])
            s3 = self.wload(w3[:, :, c0:c0 + nc_], [8, nc_])
            for _ in range(1):
                if w2_i < len(w2_parts):
                    j0, nj = w2_parts[w2_i]
                    w2_i += 1
                    dst = self.w2[:, j0:j0 + nj, :]
                    op("pool", lambda e, dst=dst, j0=j0, nj=nj: e.dma_start(out=dst, in_=w2v[:, j0:j0 + nj, :]),
                       writes=[dst], slot="w2_%d" % (w2_i - 1))
            for h in range(self.NH):
                for jj in range(nc_ // 128):
                    j = c0 // 128 + jj
                    sl = slice(h * 512, (h + 1) * 512)
                    a = self.bank(2 * (pa % 2))
                    b = self.bank(2 * (pa % 2) + 1)
                    pa += 1
                    for k in range(8):
                        op("pe", lambda e, a=a, s1=s1, k=k, jj=jj, sl=sl: e.matmul(
                            a, lhsT=s1[:, k, jj * 128:(jj + 1) * 128], rhs=self.hT[:, k, sl],
                            start=(k == 0), stop=(k == 7)),
                           reads=[s1, self.hT[:, k, sl]], writes=[a], signal=(k == 7))
                    for k in range(8):
                        op("pe", lambda e, b=b, s3=s3, k=k, jj=jj, sl=sl: e.matmul(
                            b, lhsT=s3[:, k, jj * 128:(jj + 1) * 128], rhs=self.hT[:, k, sl],
                            start=(k == 0), stop=(k == 7)),
                           reads=[s3, self.hT[:, k, sl]], writes=[b], signal=(k == 7))
                    t = self.tmp()
                    op("act", lambda e, t=t, a=a: e.activation(out=t, in_=a, func=AF.Silu), reads=[a], writes=[t])
                    op("dve", lambda e, t=t, b=b, j=j, sl=sl: e.tensor_tensor(
                        out=self.hid[:, j, sl], in0=t, in1=b, op=ALU.mult),
                       reads=[t, b], writes=[self.hid[:, j, sl]])
        while w2_i < len(w2_parts):
            j0, nj = w2_parts[w2_i]
            w2_i += 1
            dst = self.w2[:, j0:j0 + nj, :]
            op("pool", lambda e, dst=dst, j0=j0, nj=nj: e.dma_start(out=dst, in_=w2v[:, j0:j0 + nj, :]),
               writes=[dst], slot="w2_%d" % (w2_i - 1))
        pb = 0
        for h in range(self.NH):
            for m in range(8):
                sl = slice(h * 512, (h + 1) * 512)
                o = self.bank(4 + pb % 2)
                pb += 1
                for j in range(22):
                    op("pe", lambda e, o=o, j=j, m=m, sl=sl: e.matmul(
                        o, lhsT=self.w2[:, j, m * 128:(m + 1) * 128], rhs=self.hid[:, j, sl],
                        start=(j == 0), stop=(j == 21)),
                       reads=[self.w2[:, j, m * 128:(m + 1) * 128], self.hid[:, j, sl]], writes=[o],
                       signal=(j == 21))
                op("dve", lambda e, o=o, m=m, sl=sl: e.scalar_tensor_tensor(
                    out=self.xT[:, m, sl], in0=o, scalar=0.5, in1=self.xT[:, m, sl],
                    op0=ALU.mult, op1=ALU.add),
                   reads=[o, self.xT[:, m, sl]], writes=[self.xT[:, m, sl]])
            self.tail_norm(h)

    def proj_out(self, wname, src):
        op, w = self.op, self.w
        wv = self.wview(w[wname][0], 0, 1024)
        blks = [self.wload(wv[:, :, cb * 512:(cb + 1) * 512], [8, 512]) for cb in range(2)]
        pb = 0
        for h in range(self.NH):
            sl = slice(h * 512, (h + 1) * 512)
            for m in range(8):
                s = blks[m // 4]
                mm = m % 4
                o = self.bank(4 + pb % 2)
                pb += 1
                for k in range(8):
                    op("pe", lambda e, o=o, s=s, k=k, mm=mm, sl=sl: e.matmul(
                        o, lhsT=s[:, k, mm * 128:(mm + 1) * 128], rhs=src(k, sl),
                        start=(k == 0), stop=(k == 7)),
                       reads=[s, src(k, sl)], writes=[o], signal=(k == 7))
                op("dve", lambda e, o=o, m=m, sl=sl: e.tensor_tensor(
                    out=self.xT[:, m, sl], in0=o, in1=self.xT[:, m, sl], op=ALU.add),
                   reads=[o, self.xT[:, m, sl]], writes=[self.xT[:, m, sl]])
            self.tail_norm(h)

    def ple(self, l, ti):
        op, T, w = self.op, self.T, self.w
        self.norm("ple_norm", l * 8, self.hT)
        src = self.p_d[l, ti * T:(ti + 1) * T, :].rearrange("(g p) d -> p g d", p=128)
        op("sync", lambda e: e.dma_start(out=self.pst, in_=src), writes=[self.pst], slot="pst")
        for g in range(self.NG):
            ps = self.bank(g % 2)[:, 0:256]
            for c in range(2):
                op("pe", lambda e, ps=ps, g=g, c=c: e.transpose(
                    out=ps[:, c * 128:(c + 1) * 128], in_=self.pst[:, g, c * 128:(c + 1) * 128],
                    identity=self.ident),
                   reads=[self.pst[:, g, :], self.ident], writes=[ps], signal=(c == 1))
            dst = self.pT[:, :, g * 128:(g + 1) * 128]
            op("act", lambda e, dst=dst, ps=ps: e.copy(out=dst, in_=ps.rearrange("p (c t) -> p c t", c=2)),
               reads=[ps], writes=[dst])
        wg = self.wview(w["ple_gate"][l], 0, 1024)
        wp = self.wview(w["ple_proj"][l], 0, 1024)
        sp = self.wload(wp, [2, 1024])
        blks = [self.wload(wg[:, :, cb * 512:(cb + 1) * 512], [8, 512]) for cb in range(2)]
        pb = 0
        for h in range(self.NH):
            sl = slice(h * 512, (h + 1) * 512)
            for m in range(8):
                s = blks[m // 4]
                mm = m % 4
                gps = self.bank(2 * (pb % 2))
                pps = self.bank(2 * (pb % 2) + 1)
                pb += 1
                for k in range(8):
                    op("pe", lambda e, gps=gps, s=s, k=k, mm=mm, sl=sl: e.matmul(
                        gps, lhsT=s[:, k, mm * 128:(mm + 1) * 128], rhs=self.hT[:, k, sl],
                        start=(k == 0), stop=(k == 7)),
                       reads=[s, self.hT[:, k, sl]], writes=[gps], signal=(k == 7))
                for c in range(2):
                    op("pe", lambda e, pps=pps, c=c, m=m, sl=sl: e.matmul(
                        pps, lhsT=sp[:, c, m * 128:(m + 1) * 128], rhs=self.pT[:, c, sl],
                        start=(c == 0), stop=(c == 1)),
                       reads=[sp, self.pT[:, c, sl]], writes=[pps], signal=(c == 1))
                t = self.tmp()
                op("act", lambda e, t=t, gps=gps: e.activation(out=t, in_=gps, func=AF.Sigmoid),
                   reads=[gps], writes=[t])
                op("dve", lambda e, t=t, pps=pps: e.tensor_tensor(out=t, in0=t, in1=pps, op=ALU.mult),
                   reads=[t, pps], writes=[t])
                op("dve", lambda e, t=t, m=m, sl=sl: e.tensor_tensor(
                    out=self.xT[:, m, sl], in0=t, in1=self.xT[:, m, sl], op=ALU.add),
                   reads=[t, self.xT[:, m, sl]], writes=[self.xT[:, m, sl]])
            self.tail_norm(h)

    def final(self, ti, do_norm=True):
        op, T = self.op, self.T
        if do_norm:
            self.norm("final_norm", 0, self.Yf)
            src = self.Yf
        else:
            src = self.xT
        dst_d = self.y_d[ti * T:(ti + 1) * T, :].rearrange("(g p) d -> p g d", p=128)
        k = 0
        for g in range(self.NG):
            for cq in range(2):
                ps = self.bank(k % 2)
                k += 1
                for c in range(4):
                    cc = cq * 4 + c
                    op("pe", lambda e, ps=ps, g=g, c=c, cc=cc: e.transpose(
                        out=ps[:, c * 128:(c + 1) * 128], in_=src[:, cc, g * 128:(g + 1) * 128],
                        identity=self.ident),
                       reads=[src[:, cc, g * 128:(g + 1) * 128], self.ident], writes=[ps], signal=(c == 3))
                dst = self.Ys[:, g, cq * 512:(cq + 1) * 512]
                if k % 2:
                    op("act", lambda e, dst=dst, ps=ps: e.copy(out=dst, in_=ps), reads=[ps], writes=[dst])
                else:
                    op("dve", lambda e, dst=dst, ps=ps: e.tensor_copy(out=dst, in_=ps), reads=[ps], writes=[dst])
            op("sync", lambda e, g=g: e.dma_start(out=dst_d[:, g, :], in_=self.Ys[:, g, :]),
               reads=[self.Ys[:, g, :]], slot="yo%d" % g)

    def build(self):
        self.declare()
        self.setup()
        self.epsc = self.view(self.alloc(4), [1], F32)
        self.op("dve", lambda e: e.memset(self.epsc, EPS), writes=[self.epsc])
        self.onec = self.view(self.alloc(4), [1], F32)
        self.op("dve", lambda e: e.memset(self.onec, 1.0), writes=[self.onec])
        self.prologue()
        stop = self.stop_after
        for ti in range(self.NT):
            self.load_x(ti)
            phases = []
            for l in range(2):
                phases += [("ffn", l, 0), ("mix", l, 0), ("ffn", l, 1), ("ple", l, 0)]
            done_all = True
            def nparams(ph):
                kind, l, i = ph
                if kind == "ffn":
                    return ("ffn_norm", (l * 2 + i) * 8, self.hT)
                if kind == "mix":
                    return ("mix_norm", l * 8, self.hT)
                return ("ple_norm", l * 8, self.hT)
            for pi, (kind, l, i) in enumerate(phases):
                if stop is not None and pi >= stop:
                    done_all = (stop == -1)
                    break
                if pi + 1 < len(phases):
                    self.next_norm = nparams(phases[pi + 1]) if (stop is None or pi + 1 < stop) else None
                else:
                    self.next_norm = ("final_norm", 0, self.Yf)
                if kind == "ffn":
                    self.ffn(l, i)
                elif kind == "mix":
                    if l == 0:
                        self.mixer0()
                    else:
                        self.mixer1()
                else:
                    self.ple(l, ti)
            self.final(ti, do_norm=done_all)
        self.S.wait_all("sync")
        self.S.emit()

    def run_streams(self, gens):
        active = list(gens)
        flags = set()
        waiting = {}
        while active:
            progressed = False
            for g in list(active):
                w = waiting.get(id(g))
                if w is not None:
                    if w not in flags:
                        continue
                    waiting[id(g)] = None
                try:
                    r = next(g)
                except StopIteration:
                    active.remove(g)
                    progressed = True
                    continue
                progressed = True
                if r is not None:
                    kind, name = r
                    if kind == "set":
                        flags.add(name)
                    elif name not in flags:
                        waiting[id(g)] = name
            assert progressed, "stream deadlock"

    def fm_proj(self, s, col0, nchunks, dst_fn, chunk0=0, banks=(0, 1)):
        op = self.op
        for ci in range(nchunks):
            for h in range(self.NH):
                sl = slice(h * 512, (h + 1) * 512)
                ps = self.bank(banks[self.pbk % len(banks)])
                self.pbk += 1
                for k in range(8):
                    op("pe", lambda e, ps=ps, k=k, ci=ci, sl=sl: e.matmul(
                        ps, lhsT=s[:, k, col0 + ci * 128:col0 + (ci + 1) * 128], rhs=self.hT[:, k, sl],
                        start=(k == 0), stop=(k == 7)),
                       reads=[s, self.hT[:, k, sl]], writes=[ps], signal=(k == 7))
                dst_fn(chunk0 + ci, h, sl, ps)
                yield

    def tm_proj(self, s, col0, ncols, tg, ps):
        op = self.op
        for k in range(8):
            op("pe", lambda e, k=k: e.matmul(
                ps, lhsT=self.hT[:, k, tg * 128:(tg + 1) * 128], rhs=s[:, k, col0:col0 + ncols],
                start=(k == 0), stop=(k == 7)),
               reads=[s, self.hT[:, k, tg * 128:(tg + 1) * 128]], writes=[ps], signal=(k == 7))

    def decay_item(self, lf, kf, ncols, tg, ngroups, kd_dst, totv, escale, ups, et):
        op = self.op
        op("pe", lambda e: e.matmul(ups, lhsT=self.U, rhs=lf, start=True, stop=True),
           reads=[lf, self.U], writes=[ups])
        for gi in range(ngroups):
            dst = totv[:, gi, 2 * tg:2 * tg + 2]
            op("pe", lambda e, gi=gi, dst=dst: e.matmul(dst, lhsT=lf[:, gi * 128:(gi + 1) * 128], rhs=self.E,
                                                 start=True, stop=True),
               reads=[lf, self.E], writes=[dst], signal=(gi == ngroups - 1))
        op("act", lambda e: e.activation(out=et, in_=ups, func=AF.Exp, scale=escale),
           reads=[ups], writes=[et])
        op("dve", lambda e: e.tensor_tensor(out=kd_dst, in0=kf, in1=et, op=ALU.mult),
           reads=[kf, et], writes=[kd_dst])

    def gla_core(self, items, hpg, kd, qT, kdec, vtm, gT, sgT, hn, sbase, banksets, rhp, pipelined, dstf):
        op, T = self.op, self.T
        sdt = self.stat_dt
        ones = self.ones if sdt == F32 else self.ones_b
        prevs = {}

        def banks_of(it):
            pb, ob, sb_ = banksets[it % len(banksets)]
            PP = self.PS[:, pb * 512:(pb + 2) * 512].rearrange("p (s c v) -> p s c v", s=2, c=4)
            return PP, [self.bank(x) for x in ob], [self.bank(x) for x in sb_]

        def kvmm(it):
            gi, hf = items[it]
            PP, oTs, sts = banks_of(it)
            for c in range(8):
                cc = hf * 8 + c
                tg, sub = cc // 2, cc % 2
                for e_ in range(hpg):
                    head = gi * hpg + e_
                    lhsT = kdec[sub * 64:(sub + 1) * 64, tg, head * kd:(head + 1) * kd]
                    rhs = vtm[sub * 64:(sub + 1) * 64, tg, head * 128:(head + 1) * 128]
                    out = PP[e_ * kd:(e_ + 1) * kd, sub, c // 2, :]
                    op("pe", lambda e, out=out, lhsT=lhsT, rhs=rhs: e.matmul(out, lhsT=lhsT, rhs=rhs,
                                                                       start=True, stop=True),
                       reads=[lhsT, rhs], writes=[out], signal=(e_ == hpg - 1))

        def chain_gen(it, omm):
            gi, hf = items[it]
            PP, oTs, sts = banks_of(it)
            prev = prevs.get(gi, self.Sst[:, sbase + gi, :])
            for c in range(8):
                cc = hf * 8 + c
                sub = cc % 2
                new = self.Str[:, self.str_i % 8, :]
                sb = self.Sb[:, self.str_i % 8, :]
                self.str_i += 1
                pin = PP[:, sub, c // 2, :]
                gcol = gT[:, gi, cc:cc + 1]
                op("dve", lambda e, new=new, prev=prev, pin=pin, gcol=gcol: e.scalar_tensor_tensor(
                    out=new, in0=prev, scalar=gcol, in1=pin, op0=ALU.mult, op1=ALU.add),
                   reads=[prev, pin, gT], writes=[new])
                op("act", lambda e, sb=sb, new=new: e.copy(out=sb, in_=new), reads=[new], writes=[sb])
                for e_ in range(hpg):
                    out = oTs[e_][:, c * 64:(c + 1) * 64]
                    lhsT = sb[e_ * kd:(e_ + 1) * kd, :]
                    rhs = qT[e_ * kd:(e_ + 1) * kd, gi, cc * 64:(cc + 1) * 64]
                    omm.append((out, lhsT, rhs, e_ == hpg - 1))
                prev = new
                yield
            prevs[gi] = prev
            if hf == self.NH - 1:
                dstS = self.Sst[:, sbase + gi, :]
                op("dve", lambda e, dstS=dstS, prev=prev: e.tensor_copy(out=dstS, in_=prev),
                   reads=[prev], writes=[dstS])

        def chain(it):
            omm = []
            for _ in chain_gen(it, omm):
                pass
            return omm

        def chain2(ita, itb):
            oa, ob = [], []
            ga, gb = chain_gen(ita, oa), chain_gen(itb, ob)
            da = db = False
            while not (da and db):
                na, nb = len(oa), len(ob)
                if not da:
                    try:
                        next(ga)
                    except StopIteration:
                        da = True
                if not db:
                    try:
                        next(gb)
                    except StopIteration:
                        db = True
                o_mm(oa[max(0, na - hpg):na] + ob[max(0, nb - hpg):nb])
            return []

        def o_mm(omm):
            for out, lhsT, rhs, sig in omm:
                op("pe", lambda e, out=out, lhsT=lhsT, rhs=rhs: e.matmul(out, lhsT=lhsT, rhs=rhs,
                                                                   start=True, stop=True),
                   reads=[lhsT, rhs], writes=[out], signal=sig)

        hstate = {}

        def h1(it):
            gi, hf = items[it]
            PP, oTs, sts = banks_of(it)
            rhs_ = []
            for e_ in range(hpg):
                head = gi * hpg + e_
                oT = oTs[e_]
                sq = self.sq[head % 2] if sdt == F32 else self.sq[head % 2].bitcast(BF16)[:, 0:512]
                op("act", lambda e, sq=sq, oT=oT: e.activation(out=sq, in_=oT, func=AF.Square),
                   reads=[oT], writes=[sq])
                st = sts[e_]
                op("pe", lambda e, st=st, sq=sq: e.matmul(st, lhsT=ones, rhs=sq, start=True, stop=True),
                   reads=[sq, ones], writes=[st])
                rsb = self.rsb[head % 2]
                op("act", lambda e, st=st, rsb=rsb: e.activation(out=rsb, in_=st, func=AF.Ln, scale=1.0 / 128,
                                                               bias=self.epsc),
                   reads=[st], writes=[rsb])
                rh = rhp[self.rh_i % len(rhp)]
                self.rh_i += 1
                op("act", lambda e, rh=rh, rsb=rsb: e.activation(out=rh, in_=rsb, func=AF.Exp, scale=-0.5),
                   reads=[rsb], writes=[rh])
                rhs_.append(rh)
            hstate[it] = rhs_

        def h2(it):
            gi, hf = items[it]
            sl = slice(hf * 512, (hf + 1) * 512)
            PP, oTs, sts = banks_of(it)
            for e_ in range(hpg):
                head = gi * hpg + e_
                oT = oTs[e_]
                rh = hstate[it][e_]
                op("dve", lambda e, rh=rh, oT=oT: e.tensor_tensor(out=rh, in0=oT, in1=rh, op=ALU.mult),
                   reads=[oT, rh], writes=[rh])
                dst = dstf(head, sl)
                op("dve", lambda e, rh=rh, dst=dst, head=head, sl=sl: e.scalar_tensor_tensor(
                    out=dst, in0=rh, scalar=hn, in1=sgT[:, head, sl], op0=ALU.mult, op1=ALU.mult),
                   reads=[rh, sgT[:, head, sl], self.vecT], writes=[dst])

        n = len(items)
        if pipelined == 2:
            assert n % 2 == 0 and len(banksets) == 2
            npair = n // 2
            kvmm(0)
            kvmm(1)
            for p in range(npair):
                a, b = 2 * p, 2 * p + 1
                if p > 0:
                    h2(a - 2)
                    h2(b - 2)
                omm = chain2(a, b)
                if p + 1 < npair:
                    kvmm(a + 2)
                    kvmm(b + 2)
                o_mm(omm)
                h1(a)
                h1(b)
                yield
            h2(n - 2)
            h2(n - 1)
            yield
        elif pipelined:
            kvmm(0)
            for it in range(n):
                omm = chain(it)
                if it > 0:
                    h2(it - 1)
                if it + 1 < n:
                    kvmm(it + 1)
                o_mm(omm)
                h1(it)
                yield
            h2(n - 1)
            yield
        else:
            for it in range(n):
                kvmm(it)
                omm = chain(it)
                yield
                yield
                o_mm(omm)
                yield
                h1(it)
                h2(it)
                yield

    def mixer1(self):
        op, T, w, NG = self.op, self.T, self.w, self.NG
        self.norm("mix_norm", 8, self.hT)
        o = self.R1
        qT = self.view(o, [8, T], BF16); o += 8 * T * 2
        kdec = self.view(o, [NG, 1024], BF16); o += NG * 2048
        vtm = self.view(o, [NG, 1024], BF16); o += NG * 2048
        sgT = self.view(o, [8, T], F32); o += 8 * T * 4
        gT = self.view(o, [8, 16], F32); o += 512
        rt = []
        while o + 2048 <= self.W2o + 22 * T * 2 and len(rt) < 3:
            rt.append(self.view(o, [512], F32)); o += 2048
        assert len(rt) == 3
        etp = rt[2:3]
        rhp = [self.rstd[:, 0:512], self.rstd[:, 512:1024]]
        tm_ = self.tmps
        fl = [(tm_[0], tm_[1]), (tm_[2], tm_[3]), (rt[0], rt[1])]
        zb = (0, 1, 3)
        self.pbk = 0
        self.str_i = 0
        self.rh_i = 0
        wv = self.wview(w["m1_w_in"][0], 0, 4096)
        totps = self.bank(7)[:, 0:128]
        totv = totps.rearrange("p (g c) -> p g c", g=8)
        gTf = gT.rearrange("p g c -> p (g c)")
        hn = self.vcol("hgrn_head_norm", 0)
        tm = self.tmps

        def P():
            k = 0
            for cb in range(2):
                s = self.wload(wv[:, :, cb * 512:(cb + 1) * 512], [8, 512])
                def evq(ci, h, sl, ps):
                    op("act", lambda e: e.activation(out=qT[:, ci, sl], in_=ps, func=AF.Silu),
                       reads=[ps], writes=[qT[:, ci, sl]])
                yield from self.fm_proj(s, 0, 4, evq, chunk0=cb * 4)
                s = self.wload(wv[:, :, 3072 + cb * 512:3072 + (cb + 1) * 512], [8, 512])
                def evg(ci, h, sl, ps):
                    op("act", lambda e: e.activation(out=sgT[:, ci, sl], in_=ps, func=AF.Silu),
                       reads=[ps], writes=[sgT[:, ci, sl]])
                yield from self.fm_proj(s, 0, 4, evg, chunk0=cb * 4)
            for cb in range(2):
                s = self.wload(wv[:, :, 1024 + cb * 512:1024 + (cb + 1) * 512], [8, 512])
                cols = slice(cb * 512, (cb + 1) * 512)
                pend = []
                for tg in range(NG):
                    ps = self.bank(zb[k % 3])
                    self.tm_proj(s, 0, 512, tg, ps)
                    f, lf = fl[k % 3]
                    op("act", lambda e, f=f, ps=ps: e.activation(out=f, in_=ps, func=AF.Exp, scale=-1.0),
                       reads=[ps], writes=[f])
                    op("dve", lambda e, f=f, lf=lf, cols=cols: e.tensor_tensor(out=lf, in0=f, in1=self.lb_b[:, cols], op=ALU.mult),
                       reads=[f, self.lb_b[:, cols]], writes=[lf])
                    op("act", lambda e, f=f: e.activation(out=f, in_=f, func=AF.Ln, bias=self.onec),
                       reads=[f], writes=[f])
                    op("act", lambda e, lf=lf: e.activation(out=lf, in_=lf, func=AF.Ln, bias=self.onec),
                       reads=[lf], writes=[lf])
                    op("dve", lambda e, f=f, lf=lf: e.tensor_tensor(out=lf, in0=lf, in1=f, op=ALU.subtract),
                       reads=[f, lf], writes=[lf])
                    op("dve", lambda e, f=f, ps=ps: e.tensor_tensor(out=f, in0=f, in1=ps, op=ALU.add),
                       reads=[f, ps], writes=[f])
                    op("act", lambda e, f=f: e.activation(out=f, in_=f, func=AF.Exp, scale=-1.0),
                       reads=[f], writes=[f])
                    op("dve", lambda e, f=f, cols=cols: e.tensor_tensor(out=f, in0=f, in1=self.oml_b[:, cols], op=ALU.mult),
                       reads=[f, self.oml_b[:, cols]], writes=[f])
                    tv = totv[:, cb * 4:(cb + 1) * 4, :]
                    pend.append((lf, f, 512, tg, 4, kdec[:, tg, cols], tv, 1.0, self.bank(2), etp[0]))
                    if len(pend) > 2:
                        self.decay_item(*pend.pop(0))
                    k += 1
                    yield
                while pend:
                    self.decay_item(*pend.pop(0))
                op("act", lambda e, cb=cb: e.activation(out=gTf[:, cb * 64:(cb + 1) * 64],
                                                        in_=totps[:, cb * 64:(cb + 1) * 64], func=AF.Exp),
                   reads=[totps], writes=[gT])
                s = self.wload(wv[:, :, 2048 + cb * 512:2048 + (cb + 1) * 512], [8, 512])
                for tg in range(NG):
                    ps = self.bank(zb[tg % 3])
                    self.tm_proj(s, 0, 512, tg, ps)
                    dst = vtm[:, tg, cols]
                    if tg % 2:
                        op("act", lambda e, dst=dst, ps=ps: e.copy(out=dst, in_=ps), reads=[ps], writes=[dst])
                    else:
                        op("dve", lambda e, dst=dst, ps=ps: e.tensor_copy(out=dst, in_=ps), reads=[ps], writes=[dst])
                    yield
                yield ("set", "cb%d" % cb)

        def dstf(head, sl):
            return qT[:, head, sl]

        def C():
            yield ("wait", "cb0")
            setA = (4, [6], [4])
            itemsA = [(gi, hf) for gi in range(0, 4) for hf in range(self.NH)]
            yield from self.gla_core(itemsA, 1, 128, qT, kdec, vtm, gT, sgT, hn, 2, [setA], rhp, False, dstf)
            yield ("wait", "cb1")
            setA2 = (4, [6], [7])
            setB2 = (0, [2], [3])
            itemsB = []
            for g0 in (4, 6):
                for hf in range(self.NH):
                    itemsB += [(g0, hf), (g0 + 1, hf)]
            yield from self.gla_core(itemsB, 1, 128, qT, kdec, vtm, gT, sgT, hn, 2, [setA2, setB2],
                                     rhp + list(self.tmps[0:2]), 2, dstf)

        self.run_streams([P(), C()])
        self.proj_out("m1_w_out", lambda k, sl: qT[:, k, sl])

    def mixer0(self):
        op, T, w, NG = self.op, self.T, self.w, self.NG
        self.norm("mix_norm", 0, self.hT)
        o = self.R1
        qT = self.view(o, [2, T], BF16); o += 2 * T * 2
        lrT = self.view(o, [T], BF16); o += T * 2
        kdec = self.view(o, [NG, 256], BF16); o += NG * 512
        vtm = self.view(o, [NG, 512], BF16); o += NG * 1024
        sgT = self.view(o, [4, T], F32); o += 4 * T * 4
        ggT = self.view(o, [4, T], F32); o += 4 * T * 4
        xrT = self.view(o, [4, T + 4], F32)
        yb = [self.view(o + c * (T + 4) * 4, [T], BF16) for c in range(4)]
        o += 4 * (T + 4) * 4
        gT = self.view(o, [2, 16], F32); o += 128
        xc = self.view(o, [T], F32); o += T * 4
        xcb = self.view(o, [T], BF16); o += T * 2
        lt = []
        for _ in range(6):
            lt.append(self.view(o, [512], F32)); o += 2048
        rhp = [self.rstd[:, 0:512], self.rstd[:, 512:1024]]
        assert o <= self.W2o + 22 * T * 2, o
        self.pbk = 0
        self.str_i = 0
        self.rh_i = 0
        wv = self.wview(w["m0_w_in"][0], 0, M0C)
        tm = self.tmps
        totps = self.bank(3)[:, 0:32]
        totv = totps.rearrange("p (g c) -> p g c", g=2)
        hn = self.vcol("gla_head_norm", 0)

        def P():
            sB4 = self.wload(wv[:, :, 1552:2064], [8, 512])
            for c in range(4):
                op("dve", lambda e, c=c: e.tensor_copy(out=xrT[:, c, 0:3], in_=self.xr_carry[:, c, 0:3]),
                   reads=[self.xr_carry], writes=[xrT[:, c, 0:3]])
            def evxr(ci, h, sl, ps):
                dst = xrT[:, ci, 3 + h * 512:3 + (h + 1) * 512]
                op("act", lambda e: e.copy(out=dst, in_=ps), reads=[ps], writes=[dst])
            yield from self.fm_proj(sB4, 0, 4, evxr)
            yield ("set", "xr")
            sB5 = self.wload(wv[:, :, 2064:2576], [8, 512])
            def evxg(ci, h, sl, ps):
                t = tm[3]
                op("act", lambda e: e.activation(out=t, in_=ps, func=AF.Square), reads=[ps], writes=[t])
                op("dve", lambda e: e.tensor_scalar(out=t, in0=t, scalar1=0.044715, scalar2=1.0, op0=ALU.mult, op1=ALU.add),
                   reads=[t], writes=[t])
                op("dve", lambda e: e.tensor_tensor(out=t, in0=t, in1=ps, op=ALU.mult), reads=[t, ps], writes=[t])
                op("act", lambda e: e.activation(out=t, in_=t, func=AF.Sigmoid, scale=1.5957691216057308),
                   reads=[t], writes=[t])
                op("dve", lambda e: e.tensor_tensor(out=ggT[:, ci, sl], in0=t, in1=ps, op=ALU.mult),
                   reads=[t, ps], writes=[ggT[:, ci, sl]])
            yield from self.fm_proj(sB5, 0, 4, evxg)
            yield ("set", "gg")
            sB0 = self.wload(wv[:, :, 0:512], [8, 512])
            sLR = self.wload(wv[:, :, 1536:1552], [8, 16])
            sB1 = self.wload(wv[:, :, 512:1024], [8, 512])
            def evq(ci, h, sl, ps):
                op("act", lambda e: e.mul(out=qT[:, ci, sl], in_=ps, mul=0.125), reads=[ps], writes=[qT[:, ci, sl]])
            yield from self.fm_proj(sB0, 0, 2, evq)
            for h in range(self.NH):
                sl = slice(h * 512, (h + 1) * 512)
                ps = self.bank(h % 2)
                for k in range(8):
                    op("pe", lambda e, ps=ps, k=k, sl=sl: e.matmul(ps[0:16, :], lhsT=sLR[:, k, 0:16], rhs=self.hT[:, k, sl],
                                                             start=(k == 0), stop=(k == 7)),
                       reads=[sLR, self.hT[:, k, sl]], writes=[ps], signal=(k == 7))
                op("act", lambda e, ps=ps, sl=sl: e.copy(out=lrT[0:16, sl], in_=ps[0:16, :]), reads=[ps], writes=[lrT[0:16, sl]])
            yield
            pend = None
            for tg in range(NG):
                bk = self.bank(tg % 2)
                lps = bk[:, 0:256]
                kps = self.bank(4)[:, 0:256]
                op("pe", lambda e, lps=lps, tg=tg: e.matmul(lps, lhsT=lrT[0:16, tg * 128:(tg + 1) * 128],
                                                      rhs=self.gup[0:16, :], start=True, stop=True),
                   reads=[lrT[0:16, tg * 128:(tg + 1) * 128], self.gup], writes=[lps])
                sp = tm[tg % 2][:, 0:256]
                op("dve", lambda e, sp=sp, lps=lps: e.tensor_tensor(out=sp, in0=lps, in1=self.gbias_b, op=ALU.add),
                   reads=[lps, self.gbias_b], writes=[sp])
                op("act", lambda e, sp=sp: e.activation(out=sp, in_=sp, func=AF.Exp, scale=-1.0), reads=[sp], writes=[sp])
                op("act", lambda e, sp=sp: e.activation(out=sp, in_=sp, func=AF.Ln, bias=self.onec), reads=[sp], writes=[sp])
                kf = tm[tg % 2][:, 256:512]
                self.tm_proj(sB0, 256, 256, tg, kps)
                op("act", lambda e, kf=kf, kps=kps: e.copy(out=kf, in_=kps), reads=[kps], writes=[kf])
                vps = self.bank(5)
                self.tm_proj(sB1, 0, 512, tg, vps)
                dst = vtm[:, tg, :]
                op("dve", lambda e, dst=dst, vps=vps: e.tensor_copy(out=dst, in_=vps), reads=[vps], writes=[dst])
                if pend is not None:
                    self.decay_item(*pend)
                pend = (sp, kf, 256, tg, 2, kdec[:, tg, :], totv, -1.0 / 16.0, self.bank(2)[:, 0:256], tm[2][:, 0:256])
                yield
            self.decay_item(*pend)
            op("act", lambda e: e.activation(out=gT.rearrange("p g c -> p (g c)"), in_=totps, func=AF.Exp,
                                             scale=-1.0 / 16.0), reads=[totps], writes=[gT])
            sB2 = self.wload(wv[:, :, 1024:1536], [8, 512])
            def evg(ci, h, sl, ps):
                op("act", lambda e: e.activation(out=sgT[:, ci, sl], in_=ps, func=AF.Silu), reads=[ps], writes=[sgT[:, ci, sl]])
            yield from self.fm_proj(sB2, 0, 4, evg)
            yield ("set", "kv")

        def L():
            yield ("wait", "xr")
            yield ("wait", "gg")
            its = [(c, h) for c in range(4) for h in range(self.NH)]

            def conv(c):
                w0 = self.vcol("lru_conv_w", 0 * 4 + c)
                cb_ = self.vcol("lru_conv_b", c)
                op("dve", lambda e: e.tensor_scalar(out=xc, in0=xrT[:, c, 0:T], scalar1=w0, scalar2=cb_,
                                                    op0=ALU.mult, op1=ALU.add),
                   reads=[xrT[:, c, :], self.vecT], writes=[xc])
                for j in range(1, 4):
                    wj = self.vcol("lru_conv_w", j * 4 + c)
                    op("dve", lambda e, j=j, wj=wj: e.scalar_tensor_tensor(
                        out=xc, in0=xrT[:, c, j:j + T], scalar=wj, in1=xc, op0=ALU.mult, op1=ALU.add),
                       reads=[xrT[:, c, :], xc, self.vecT], writes=[xc])
                op("dve", lambda e: e.tensor_copy(out=self.xr_carry[:, c, 0:3], in_=xrT[:, c, T:T + 3]),
                   reads=[xrT[:, c, :]], writes=[self.xr_carry])
                op("act", lambda e: e.copy(out=xcb, in_=xc), reads=[xc], writes=[xcb])

            def stageA(k):
                c, h = its[k]
                sl = slice(h * 512, (h + 1) * 512)
                a_, e2, u_ = lt[3 * (k % 2)], lt[3 * (k % 2) + 1], lt[3 * (k % 2) + 2]
                ra = self.bank(6)
                ia = self.bank(7)
                ba = self.vcol("lru_ba", c)
                bx = self.vcol("lru_bx", c)
                op("pe", lambda e: e.matmul(ra, lhsT=self.bd[:, 0, c, :], rhs=xcb[:, sl], start=True, stop=True),
                   reads=[self.bd, xcb[:, sl]], writes=[ra])
                op("pe", lambda e: e.matmul(ia, lhsT=self.bd[:, 1, c, :], rhs=xcb[:, sl], start=True, stop=True),
                   reads=[self.bd, xcb[:, sl]], writes=[ia])
                op("act", lambda e: e.activation(out=ra, in_=ra, func=AF.Sigmoid, bias=ba),
                   reads=[ra, self.vecT], writes=[ra])
                op("act", lambda e: e.activation(out=u_, in_=ia, func=AF.Sigmoid, bias=bx),
                   reads=[ia, self.vecT], writes=[u_])
                op("act", lambda e: e.activation(out=a_, in_=ra, func=AF.Exp, scale=self.clam[:, c:c + 1]),
                   reads=[ra, self.clam], writes=[a_])
                op("act", lambda e: e.activation(out=e2, in_=ra, func=AF.Exp, scale=self.clam[:, 4 + c:5 + c]),
                   reads=[ra, self.clam], writes=[e2])
                op("dve", lambda e: e.tensor_tensor(out=u_, in0=u_, in1=xc[:, sl], op=ALU.mult),
                   reads=[u_, xc[:, sl]], writes=[u_])

            def stageB(k):
                c, h = its[k]
                sl = slice(h * 512, (h + 1) * 512)
                a_, e2, u_ = lt[3 * (k % 2)], lt[3 * (k % 2) + 1], lt[3 * (k % 2) + 2]
                op("dve", lambda e: e.tensor_scalar(out=e2, in0=e2, scalar1=1.0 - 1e-6, scalar2=-1.0, op0=ALU.min, op1=ALU.mult),
                   reads=[e2], writes=[e2])
                op("act", lambda e: e.activation(out=e2, in_=e2, func=AF.Ln, bias=self.onec), reads=[e2], writes=[e2])
                op("act", lambda e: e.activation(out=e2, in_=e2, func=AF.Exp, scale=0.5), reads=[e2], writes=[e2])
                op("dve", lambda e: e.tensor_tensor(out=u_, in0=u_, in1=e2, op=ALU.mult), reads=[u_, e2], writes=[u_])
                op("dve", lambda e: e.tensor_tensor_scan(out=e2, data0=a_, data1=u_, initial=self.hprev[:, c:c + 1],
                                                        op0=ALU.mult, op1=ALU.add),
                   reads=[a_, u_, self.hprev], writes=[e2])
                op("dve", lambda e: e.tensor_copy(out=self.hprev[:, c:c + 1], in_=e2[:, 511:512]),
                   reads=[e2], writes=[self.hprev])
                dst = yb[c][:, sl]
                op("dve", lambda e: e.tensor_tensor(out=dst, in0=e2, in1=ggT[:, c, sl], op=ALU.mult),
                   reads=[e2, ggT[:, c, sl]], writes=[dst])

            for k in range(len(its)):
                if its[k][1] == 0:
                    conv(its[k][0])
                    yield
                    yield
                    yield
                    yield
                stageA(k)
                yield
                if k > 0:
                    stageB(k - 1)
                    yield
            stageB(len(its) - 1)

        def C():
            yield ("wait", "kv")
            items = [(gi, hf) for gi in range(2) for hf in range(self.NH)]
            setA = (0, [2, 3], [0, 1])
            yield from self.gla_core(items, 2, 64, qT, kdec, vtm, gT, sgT, hn, 0, [setA], rhp, False,
                                     lambda head, sl: self.hT[:, head, sl])

        self.run_streams([P(), L(), C()])
        self.proj_out("m0_w_out", lambda k, sl: (self.hT[:, k, sl] if k < 4 else yb[k - 4][:, sl]))


def build_nc(T=1024, NT=4, stop_after=None, stat_dt=BF16):
    nc = bass.Bass("TRN2", target_bir_lowering=False)
    with ExitStack() as st:
        b = Builder(nc, st, T=T, NT=NT, stop_after=stop_after, stat_dt=stat_dt)
        b.build()
        info = (b.S.nops, b.S.nwaits, b.aoff)
    return nc, info


def kernel(**inputs):
    n = 8
    nc, info = build_nc()
    consts = make_consts()
    x = np.ascontiguousarray(inputs["x"], dtype=np.float32)
    p = np.ascontiguousarray(inputs["p"], dtype=np.float32)
    wts = {k: np.ascontiguousarray(inputs[k], dtype=np.float32) for k in W_SHAPES}
    in_maps = []
    for c in range(n):
        m = {"x": x[c], "p": np.ascontiguousarray(p[:, c]), "consts": consts}
        m.update(wts)
        in_maps.append(m)
    res = run_bass_kernel_spmd(nc, in_maps, core_ids=list(range(n)))
    return np.stack([r["y"] for r in res.results], axis=0)
```

```python
import numpy as np
from contextlib import ExitStack
import concourse.bass as bass
import concourse.mybir as mybir
from concourse.bass_utils import run_bass_kernel_spmd

F32 = mybir.dt.float32
BF16 = mybir.dt.bfloat16
AF = mybir.ActivationFunctionType
ALU = mybir.AluOpType

ENGS = ("sync", "act", "pe", "dve", "pool")
D = 1024
SEQ = 4096
DFF = 2816
EPS = 1e-6
M0C = 2576


def _esz(dt):
    return 4 if dt == F32 else 2


class Sched:
    def __init__(self, nc, stack):
        self.nc = nc
        self.stack = stack
        self.streams = {e: [] for e in ENGS}
        self.sem = {}
        self.cnt = {}
        self.seen = {e: {} for e in ENGS}
        self.snap = {}
        self.last_w = {}
        self.readers = {}
        self.pending = {e: ([], []) for e in ENGS}
        self.nwaits = 0
        self.nops = 0

    def _sem(self, v):
        if v not in self.sem:
            self.sem[v] = self.stack.enter_context(self.nc.semaphore("s_" + v))
            self.cnt[v] = 0
        return self.sem[v]

    @staticmethod
    def keys(ap):
        if isinstance(ap, (str, tuple)):
            return [ap]
        name = ap.tensor.name
        pat = ap.ap
        pstride = pat[0][0]
        esz = _esz(ap.dtype)
        off = ap.offset
        p0 = off // pstride
        col = off % pstride
        ext = 1
        for st, c in pat[1:]:
            ext += (c - 1) * abs(st)
        gran = 2048 if name.startswith("PS") else 256
        b0 = (col * esz) // gran
        b1 = ((col + ext) * esz - 1) // gran
        p1 = p0 + pat[0][1] - 1
        halves = set()
        if p0 < 64:
            halves.add(0)
        if p1 >= 64:
            halves.add(1)
        return [(name, b, h) for b in range(b0, b1 + 1) for h in halves]

    def op(self, eng, fn, reads=(), writes=(), slot=None, signal=True):
        rk = [k for a in reads for k in self.keys(a)]
        wk = [k for a in writes for k in self.keys(a)]
        need = {}

        def req(tok):
            if tok is not None and need.get(tok[0], 0) < tok[1]:
                need[tok[0]] = tok[1]

        for k in rk:
            req(self.last_w.get(k))
        for k in wk:
            req(self.last_w.get(k))
            r = self.readers.get(k)
            if r:
                for v, c in r.items():
                    req((v, c))
        if slot is not None:
            pv = self.cnt.get("d_" + slot, 0)
            if pv > 0:
                req(("d_" + slot, pv))
        seen = self.seen[eng]
        waits = []
        for v, c in need.items():
            if c > self.cnt.get(v, 0):
                if v == eng:
                    continue
                print("WARNING future-wait", eng, "on", v, c, self.cnt.get(v, 0))
            if seen.get(v, 0) >= c:
                continue
            waits.append((v, c))
            seen[v] = c
            sn = self.snap.get((v, c))
            if sn:
                for v2, c2 in sn.items():
                    if seen.get(v2, 0) < c2:
                        seen[v2] = c2
        self.nwaits += len(waits)
        self.nops += 1
        veng = eng if slot is None else "d_" + slot
        inc = 1 if slot is None else 16
        self._sem(veng)
        tok = (veng, self.cnt[veng] + inc)
        for k in rk:
            self.readers.setdefault(k, {})[veng] = tok[1]
        for k in wk:
            self.last_w[k] = tok
            self.readers[k] = {}
        if not signal:
            assert slot is None
            self.streams[eng].append((waits, fn, None, 0))
            return
        self.cnt[veng] += inc
        self.snap[tok] = dict(seen)
        self.streams[eng].append((waits, fn, veng, inc))

    def wait_all(self, eng):
        waits = []
        for v, c in self.cnt.items():
            if c > 0 and self.seen[eng].get(v, 0) < c:
                waits.append((v, c))
                self.seen[eng][v] = c
        self.streams[eng].append((waits, None, None, 0))

    def emit(self):
        nc = self.nc
        with nc.Block() as block:
            def run(name, e):
                for waits, fn, veng, inc in self.streams[name]:
                    for v, c in waits:
                        e.wait_ge(self.sem[v], c)
                    if fn is None:
                        continue
                    ins = fn(e)
                    if veng is not None:
                        ins.then_inc(self.sem[veng], inc)

            @block.sync
            def _(e):
                run("sync", e)

            @block.scalar
            def _(e):
                run("act", e)

            @block.tensor
            def _(e):
                run("pe", e)

            @block.vector
            def _(e):
                run("dve", e)

            @block.gpsimd
            def _(e):
                run("pool", e)


VROWS = {}
_r = 0
for _n, _k in (("ffn_norm", 32), ("mix_norm", 16), ("ple_norm", 16), ("final_norm", 8),
               ("gla_head_norm", 1), ("hgrn_head_norm", 1), ("lru_conv_w", 16),
               ("lru_conv_b", 4), ("lru_ba", 4), ("lru_bx", 4), ("lru_lambda", 4)):
    VROWS[_n] = (_r, _k)
    _r += _k
NVROWS = _r

W_SHAPES = {
    "ffn_norm": [2, 2, 1024], "ffn_w1": [2, 2, 1024, 2816], "ffn_w3": [2, 2, 1024, 2816],
    "ffn_w2": [2, 2, 2816, 1024], "mix_norm": [2, 1024], "ple_norm": [2, 1024],
    "ple_proj": [2, 256, 1024], "ple_gate": [2, 1024, 1024], "final_norm": [1024],
    "m0_w_in": [1, 1024, 2576], "gla_gate_up": [1, 16, 256], "gla_gate_bias": [1, 256],
    "gla_head_norm": [1, 128], "lru_conv_w": [1, 4, 512], "lru_conv_b": [1, 512],
    "lru_wa": [1, 8, 64, 64], "lru_ba": [1, 512], "lru_wx": [1, 8, 64, 64], "lru_bx": [1, 512],
    "lru_lambda": [1, 512], "m0_w_out": [1, 1024, 1024], "m1_w_in": [1, 1024, 4096],
    "hgrn_lb_logits": [2, 1024], "hgrn_head_norm": [1, 128], "m1_w_out": [1, 1024, 1024],
}


def make_consts():
    c = np.zeros((128, 3 * 128 + 2), np.float32)
    c[:, 0:128] = np.eye(128, dtype=np.float32)
    c[:, 128:256] = 1.0
    s = np.arange(128)[:, None]
    t = np.arange(128)[None, :]
    c[:, 256:384] = ((s > t) & ((s // 64) == (t // 64))).astype(np.float32)
    c[:, 384] = (np.arange(128) < 64)
    c[:, 385] = (np.arange(128) >= 64)
    return c


class Builder:
    def __init__(self, nc, stack, T=1024, NT=4, stop_after=None, stat_dt=F32):
        self.nc = nc
        self.st = stack
        self.T = T
        self.NT = NT
        self.NH = T // 512
        self.NG = T // 128
        self.stop_after = stop_after
        self.stat_dt = stat_dt
        self.S = Sched(nc, stack)
        self.ws_i = 0
        self._rec = None
        self.norm_done = None
        self.next_norm = None
        self.tmp_i = 0
        self.ps_rot = {}

    def view(self, off, shape, dt):
        esz = _esz(dt)
        n = int(np.prod(shape))
        assert off % 4 == 0
        a = self.arena[:, off // 2: off // 2 + n * esz // 2]
        if dt == F32:
            a = a.bitcast(F32)
        if len(shape) == 2:
            a = a.rearrange("p (a b) -> p a b", a=shape[0])
        elif len(shape) == 3:
            a = a.rearrange("p (a b c) -> p a b c", a=shape[0], b=shape[1])
        return a

    def alloc(self, nbytes):
        if nbytes >= 256:
            self.aoff = (self.aoff + 255) // 256 * 256
        off = self.aoff
        self.aoff += (nbytes + 3) // 4 * 4
        return off

    def bank(self, i):
        return self.PS[:, i * 512:(i + 1) * 512]

    def op(self, *a, **k):
        if self._rec is not None:
            self._rec.append(("op", a, k))
        else:
            self.S.op(*a, **k)

    def declare(self):
        nc = self.nc
        NTOK = self.T * self.NT
        self.x_d = nc.dram_tensor("x", [NTOK, D], F32, kind="ExternalInput").ap()
        self.p_d = nc.dram_tensor("p", [2, NTOK, 256], F32, kind="ExternalInput").ap()
        self.w = {}
        for n, shp in W_SHAPES.items():
            self.w[n] = nc.dram_tensor(n, shp, F32, kind="ExternalInput").ap()
        self.c_d = nc.dram_tensor("consts", [128, 386], F32, kind="ExternalInput").ap()
        self.y_d = nc.dram_tensor("y", [NTOK, D], F32, kind="ExternalOutput").ap()

    def setup(self):
        nc, st, T = self.nc, self.st, self.T
        ARENA = 212480
        self.arena = st.enter_context(nc.sbuf_tensor("arena", [128, ARENA // 2], BF16))
        self.PS = st.enter_context(nc.psum_tensor("PS", [128, 4096], F32))
        self.aoff = 0
        A = self.alloc
        self.xT = self.view(A(8 * T * 4), [8, T], F32)
        self.hT = self.view(A(8 * T * 2), [8, T], BF16)
        self.R1 = A(22 * T * 2)
        self.W2o = A(22 * T * 2)
        self.RING = 24576
        self.WSbase = A(self.RING)
        self.ring_off = 0
        self.sq = [self.view(A(2048), [512], F32) for _ in range(2)]
        self.rs = self.view(A(2048), [512], F32)
        self.rsb = [self.rs, self.view(A(2048), [512], F32)]
        self.rstd = self.view(A(T * 4), [T], F32)
        self.tmps = [self.view(A(2048), [512], F32) for _ in range(4)]
        self.cst = self.view(A(386 * 4), [386], F32)
        self.ident = self.cst[:, 0:128]
        self.ones = self.cst[:, 128:256]
        self.U = self.cst[:, 256:384]
        self.E = self.cst[:, 384:386]
        self.ones_b = self.view(A(256), [128], BF16)
        self.vstage = self.view(A(512), [128], F32)
        self.vecT = self.view(A(128 * 4), [128], F32)
        self.gbias_b = self.view(A(1024), [256], F32)
        self.lb_b = self.view(A(4096), [1024], F32)
        self.oml_b = self.view(A(4096), [1024], F32)
        self.gup = self.view(A(512), [256], BF16)
        self.bdf = self.view(self.R1, [2, 4, 128], F32)
        self.bd = self.view(A(2048), [2, 4, 128], BF16)
        self.clam = self.view(A(32), [8], F32)
        self.Sst = self.view(A(10 * 512), [10, 128], F32)
        self.Str = self.view(A(8 * 512), [8, 128], F32)
        self.Sb = self.view(A(8 * 256), [8, 128], BF16)
        self.hprev = self.view(A(16), [4], F32)
        self.aoff_end = self.aoff
        assert self.aoff <= ARENA, self.aoff
        R1, W2o = self.R1, self.W2o
        self.hid = self.view(R1, [22, T], BF16)
        self.w2 = self.view(W2o, [22, 1024], BF16)
        self.Xs = self.view(R1, [self.NG, 1024], F32)
        self.Yf = self.view(R1, [8, T], F32)
        self.Ys = self.view(W2o, [self.NG, 1024], F32)
        self.pst = self.view(W2o, [self.NG, 256], F32)
        self.pT = self.view(W2o + self.NG * 1024, [2, T], BF16)
        self.xr_carry = self.view(A(64), [4, 4], F32)

    def tmp(self):
        t = self.tmps[self.tmp_i % 4]
        self.tmp_i += 1
        return t

    def wload(self, src, shape):
        nbytes = int(np.prod(shape)) * 2
        if self.ring_off + nbytes > self.RING:
            self.ring_off = 0
        dst = self.view(self.WSbase + self.ring_off, shape, BF16)
        self.ring_off += nbytes
        name = "ws%d" % (self.ws_i % 8)
        self.ws_i += 1
        self.op("pool", lambda e, dst=dst, src=src: e.dma_start(out=dst, in_=src),
                writes=[dst], slot=name)
        return dst

    def wview(self, wap, c0, c1):
        return wap.rearrange("(k p) n -> p k n", p=128)[:, :, c0:c1]

    def prologue(self):
        op, w = self.op, self.w
        op("sync", lambda e: e.dma_start(out=self.cst, in_=self.c_d), writes=[self.cst], slot="c0")
        op("dve", lambda e: e.memset(self.vstage, 0.0), writes=[self.vstage])
        for i, (n, (r0, k)) in enumerate(VROWS.items()):
            src = w[n]
            shp = W_SHAPES[n]
            if len(shp) == 3:
                src = src.rearrange("a b (c p) -> (a b c) p", p=128)
            elif len(shp) == 2:
                src = src.rearrange("a (c p) -> (a c) p", p=128)
            else:
                src = src.rearrange("(c p) -> c p", p=128)
            op("sync", lambda e, src=src, r0=r0, k=k: e.dma_start(out=self.vstage[r0:r0 + k, :], in_=src),
               writes=[self.vstage], slot="v%d" % i)
        ps = self.bank(0)[:, 0:128]
        op("pe", lambda e: e.transpose(out=ps, in_=self.vstage, identity=self.ident),
           reads=[self.vstage, self.ident], writes=[ps])
        op("dve", lambda e: e.tensor_copy(out=self.vecT, in_=ps), reads=[ps], writes=[self.vecT])
        op("dve", lambda e: e.tensor_copy(out=self.ones_b, in_=self.ones), reads=[self.ones], writes=[self.ones_b])
        op("sync", lambda e: e.dma_start(out=self.gbias_b, in_=w["gla_gate_bias"][0].partition_broadcast(128)),
           writes=[self.gbias_b], slot="b0")
        t01 = self.view(self.W2o, [2, 1024], F32)
        op("sync", lambda e: e.dma_start(out=t01[:, 0, :], in_=w["hgrn_lb_logits"][0].partition_broadcast(128)),
           writes=[t01[:, 0, :]], slot="b1")
        op("sync", lambda e: e.dma_start(out=t01[:, 1, :], in_=w["hgrn_lb_logits"][1].partition_broadcast(128)),
           writes=[t01[:, 1, :]], slot="b2")
        op("dve", lambda e: e.tensor_tensor(out=t01[:, 0, :], in0=t01[:, 1, :], in1=t01[:, 0, :], op=ALU.subtract),
           reads=[t01], writes=[t01[:, 0, :]])
        op("act", lambda e: e.activation(out=self.lb_b, in_=t01[:, 0, :], func=AF.Sigmoid),
           reads=[t01[:, 0, :]], writes=[self.lb_b])
        op("dve", lambda e: e.tensor_scalar(out=self.oml_b, in0=self.lb_b, scalar1=-1.0, scalar2=1.0,
                                            op0=ALU.mult, op1=ALU.add),
           reads=[self.lb_b], writes=[self.oml_b])
        gst = self.tmps[1]
        op("sync", lambda e: e.dma_start(out=gst[0:16, 0:256], in_=w["gla_gate_up"][0]),
           writes=[gst], slot="b3")
        op("dve", lambda e: e.tensor_copy(out=self.gup[0:16, :], in_=gst[0:16, 0:256]), reads=[gst], writes=[self.gup])
        op("dve", lambda e: e.memset(self.bdf, 0.0), writes=[self.bdf])
        for wi, n in enumerate(("lru_wa", "lru_wx")):
            for g in range(8):
                h = g % 2
                dst = self.bdf[h * 64:(h + 1) * 64, wi, g // 2, h * 64:(h + 1) * 64]
                op("sync", lambda e, dst=dst, n=n, g=g: e.dma_start(out=dst, in_=w[n][0, g]),
                   writes=[self.bdf], slot="bd%d_%d" % (wi, g))
        op("dve", lambda e: e.tensor_copy(out=self.bd, in_=self.bdf), reads=[self.bdf], writes=[self.bd])
        r0 = VROWS["lru_lambda"][0]
        lam = self.vecT[:, r0:r0 + 4]
        t = self.tmps[2][:, 0:4]
        op("act", lambda e: e.activation(out=t, in_=lam, func=AF.Exp, scale=-1.0), reads=[self.vecT], writes=[t])
        op("act", lambda e: e.activation(out=t, in_=t, func=AF.Ln, bias=self.onec), reads=[t], writes=[t])
        op("dve", lambda e: e.tensor_scalar(out=self.clam[:, 0:4], in0=t, scalar1=-8.0, scalar2=None, op0=ALU.mult),
           reads=[t], writes=[self.clam])
        op("dve", lambda e: e.tensor_scalar(out=self.clam[:, 4:8], in0=t, scalar1=-16.0, scalar2=None, op0=ALU.mult),
           reads=[t], writes=[self.clam])
        op("dve", lambda e: e.memset(self.Sst, 0.0), writes=[self.Sst])
        op("dve", lambda e: e.memset(self.hprev, 0.0), writes=[self.hprev])
        op("dve", lambda e: e.memset(self.xr_carry, 0.0), writes=[self.xr_carry])

    def vcol(self, name, idx):
        r0, k = VROWS[name]
        assert idx < k
        return self.vecT[:, r0 + idx:r0 + idx + 1]

    def load_x(self, ti):
        op, T = self.op, self.T
        src = self.x_d[ti * T:(ti + 1) * T, :].rearrange("(g p) d -> p g d", p=128)
        for g in range(self.NG):
            op("sync", lambda e, g=g: e.dma_start(out=self.Xs[:, g, :], in_=src[:, g, :]),
               writes=[self.Xs[:, g, :]], slot="xs%d" % g)
        k = 0
        for g in range(self.NG):
            for cq in range(2):
                ps = self.bank(k % 2)
                k += 1
                for c in range(4):
                    cc = cq * 4 + c
                    op("pe", lambda e, ps=ps, g=g, c=c, cc=cc: e.transpose(
                        out=ps[:, c * 128:(c + 1) * 128], in_=self.Xs[:, g, cc * 128:(cc + 1) * 128],
                        identity=self.ident),
                       reads=[self.Xs[:, g, cc * 128:(cc + 1) * 128], self.ident], writes=[ps], signal=(c == 3))
                dst = self.xT[:, cq * 4:(cq + 1) * 4, g * 128:(g + 1) * 128]
                eng = "act" if (k % 2) else "dve"
                if eng == "act":
                    op("act", lambda e, dst=dst, ps=ps: e.copy(out=dst, in_=ps.rearrange("p (c t) -> p c t", c=4)),
                       reads=[ps], writes=[dst])
                else:
                    op("dve", lambda e, dst=dst, ps=ps: e.tensor_copy(out=dst, in_=ps.rearrange("p (c t) -> p c t", c=4)),
                       reads=[ps], writes=[dst])

    def norm_half(self, h, gname, gidx0, dst, src=None, stat_bank=7):
        op, T = self.op, self.T
        src = self.xT if src is None else src
        sdt = self.stat_dt
        ones = self.ones if sdt == F32 else self.ones_b
        sl = slice(h * 512, (h + 1) * 512)
        ps = self.bank(stat_bank)
        for c in range(8):
            sq = self.sq[c % 2] if sdt == F32 else self.sq[c % 2].bitcast(BF16)[:, 0:512]
            op("act", lambda e, sq=sq, c=c, sl=sl: e.activation(out=sq, in_=src[:, c, sl], func=AF.Square),
               reads=[src[:, c, sl]], writes=[sq])
            op("pe", lambda e, sq=sq, c=c, ps=ps: e.matmul(ps, lhsT=ones, rhs=sq, start=(c == 0), stop=(c == 7)),
               reads=[sq, ones], writes=[ps])
        op("act", lambda e, ps=ps: e.activation(out=self.rs, in_=ps, func=AF.Ln, scale=1.0 / D, bias=self.epsc),
           reads=[ps], writes=[self.rs])
        op("act", lambda e, sl=sl: e.activation(out=self.rstd[:, sl], in_=self.rs, func=AF.Exp, scale=-0.5),
           reads=[self.rs], writes=[self.rstd[:, sl]])
        for c in range(8):
            g = self.vcol(gname, gidx0 + c)
            op("dve", lambda e, c=c, sl=sl, g=g: e.scalar_tensor_tensor(
                out=dst[:, c, sl], in0=src[:, c, sl], scalar=g, in1=self.rstd[:, sl],
                op0=ALU.mult, op1=ALU.mult),
               reads=[src[:, c, sl], self.rstd[:, sl], self.vecT], writes=[dst[:, c, sl]])

    def norm(self, gname, gidx0, dst):
        if self.norm_done == (gname, gidx0):
            self.norm_done = None
            return
        for h in range(self.NH):
            self.norm_half(h, gname, gidx0, dst)

    def tail_norm(self, h):
        if self.next_norm is None:
            return
        gname, gidx0, dst = self.next_norm
        self.norm_half(h, gname, gidx0, dst)
        if h == self.NH - 1:
            self.norm_done = (gname, gidx0)

    def ffn(self, l, i):
        op, T, w = self.op, self.T, self.w
        self.norm("ffn_norm", (l * 2 + i) * 8, self.hT)
        w1 = self.wview(w["ffn_w1"][l, i], 0, DFF)
        w3 = self.wview(w["ffn_w3"][l, i], 0, DFF)
        w2v = w["ffn_w2"][l, i].rearrange("(j p) n -> p j n", p=128)
        blocks = [(b * 256, 256) for b in range(11)]
        w2_parts = [(j, min(2, 22 - j)) for j in range(0, 22, 2)]
        w2_i = 0
        pa = 0
        for bi, (c0, nc_) in enumerate(blocks):
            s1 = self.wload(w1[:, :, c0:c0 + nc_], [8, nc_])
            s3 = self.wload(w3[:, :, c0:c0 + nc_], [8, nc_])
            for _ in range(1):
                if w2_i < len(w2_parts):
                    j0, nj = w2_parts[w2_i]
                    w2_i += 1
                    dst = self.w2[:, j0:j0 + nj, :]
                    op("pool", lambda e, dst=dst, j0=j0, nj=nj: e.dma_start(out=dst, in_=w2v[:, j0:j0 + nj, :]),
                       writes=[dst], slot="w2_%d" % (w2_i - 1))
            for h in range(self.NH):
                for jj in range(nc_ // 128):
                    j = c0 // 128 + jj
                    sl = slice(h * 512, (h + 1) * 512)
                    a = self.bank(2 * (pa % 2))
                    b = self.bank(2 * (pa % 2) + 1)
                    pa += 1
                    for k in range(8):
                        op("pe", lambda e, a=a, s1=s1, k=k, jj=jj, sl=sl: e.matmul(
                            a, lhsT=s1[:, k, jj * 128:(jj + 1) * 128], rhs=self.hT[:, k, sl],
                            start=(k == 0), stop=(k == 7)),
                           reads=[s1, self.hT[:, k, sl]], writes=[a], signal=(k == 7))
                    for k in range(8):
                        op("pe", lambda e, b=b, s3=s3, k=k, jj=jj, sl=sl: e.matmul(
                            b, lhsT=s3[:, k, jj * 128:(jj + 1) * 128], rhs=self.hT[:, k, sl],
                            start=(k == 0), stop=(k == 7)),
                           reads=[s3, self.hT[:, k, sl]], writes=[b], signal=(k == 7))
                    t = self.tmp()
                    op("act", lambda e, t=t, a=a: e.activation(out=t, in_=a, func=AF.Silu), reads=[a], writes=[t])
                    op("dve", lambda e, t=t, b=b, j=j, sl=sl: e.tensor_tensor(
                        out=self.hid[:, j, sl], in0=t, in1=b, op=ALU.mult),
                       reads=[t, b], writes=[self.hid[:, j, sl]])
        while w2_i < len(w2_parts):
            j0, nj = w2_parts[w2_i]
            w2_i += 1
            dst = self.w2[:, j0:j0 + nj, :]
            op("pool", lambda e, dst=dst, j0=j0, nj=nj: e.dma_start(out=dst, in_=w2v[:, j0:j0 + nj, :]),
               writes=[dst], slot="w2_%d" % (w2_i - 1))
        pb = 0
        for h in range(self.NH):
            for m in range(8):
                sl = slice(h * 512, (h + 1) * 512)
                o = self.bank(4 + pb % 2)
                pb += 1
                for j in range(22):
                    op("pe", lambda e, o=o, j=j, m=m, sl=sl: e.matmul(
                        o, lhsT=self.w2[:, j, m * 128:(m + 1) * 128], rhs=self.hid[:, j, sl],
                        start=(j == 0), stop=(j == 21)),
                       reads=[self.w2[:, j, m * 128:(m + 1) * 128], self.hid[:, j, sl]], writes=[o],
                       signal=(j == 21))
                op("dve", lambda e, o=o, m=m, sl=sl: e.scalar_tensor_tensor(
                    out=self.xT[:, m, sl], in0=o, scalar=0.5, in1=self.xT[:, m, sl],
                    op0=ALU.mult, op1=ALU.add),
                   reads=[o, self.xT[:, m, sl]], writes=[self.xT[:, m, sl]])
            self.tail_norm(h)

    def proj_out(self, wname, src):
        op, w = self.op, self.w
        wv = self.wview(w[wname][0], 0, 1024)
        blks = [self.wload(wv[:, :, cb * 512:(cb + 1) * 512], [8, 512]) for cb in range(2)]
        pb = 0
        for h in range(self.NH):
            sl = slice(h * 512, (h + 1) * 512)
            for m in range(8):
                s = blks[m // 4]
                mm = m % 4
                o = self.bank(4 + pb % 2)
                pb += 1
                for k in range(8):
                    op("pe", lambda e, o=o, s=s, k=k, mm=mm, sl=sl: e.matmul(
                        o, lhsT=s[:, k, mm * 128:(mm + 1) * 128], rhs=src(k, sl),
                        start=(k == 0), stop=(k == 7)),
                       reads=[s, src(k, sl)], writes=[o], signal=(k == 7))
                op("dve", lambda e, o=o, m=m, sl=sl: e.tensor_tensor(
                    out=self.xT[:, m, sl], in0=o, in1=self.xT[:, m, sl], op=ALU.add),
                   reads=[o, self.xT[:, m, sl]], writes=[self.xT[:, m, sl]])
            self.tail_norm(h)

    def ple(self, l, ti):
        op, T, w = self.op, self.T, self.w
        self.norm("ple_norm", l * 8, self.hT)
        src = self.p_d[l, ti * T:(ti + 1) * T, :].rearrange("(g p) d -> p g d", p=128)
        op("sync", lambda e: e.dma_start(out=self.pst, in_=src), writes=[self.pst], slot="pst")
        for g in range(self.NG):
            ps = self.bank(g % 2)[:, 0:256]
            for c in range(2):
                op("pe", lambda e, ps=ps, g=g, c=c: e.transpose(
                    out=ps[:, c * 128:(c + 1) * 128], in_=self.pst[:, g, c * 128:(c + 1) * 128],
                    identity=self.ident),
                   reads=[self.pst[:, g, :], self.ident], writes=[ps], signal=(c == 1))
            dst = self.pT[:, :, g * 128:(g + 1) * 128]
            op("act", lambda e, dst=dst, ps=ps: e.copy(out=dst, in_=ps.rearrange("p (c t) -> p c t", c=2)),
               reads=[ps], writes=[dst])
        wg = self.wview(w["ple_gate"][l], 0, 1024)
        wp = self.wview(w["ple_proj"][l], 0, 1024)
        sp = self.wload(wp, [2, 1024])
        blks = [self.wload(wg[:, :, cb * 512:(cb + 1) * 512], [8, 512]) for cb in range(2)]
        pb = 0
        for h in range(self.NH):
            sl = slice(h * 512, (h + 1) * 512)
            for m in range(8):
                s = blks[m // 4]
                mm = m % 4
                gps = self.bank(2 * (pb % 2))
                pps = self.bank(2 * (pb % 2) + 1)
                pb += 1
                for k in range(8):
                    op("pe", lambda e, gps=gps, s=s, k=k, mm=mm, sl=sl: e.matmul(
                        gps, lhsT=s[:, k, mm * 128:(mm + 1) * 128], rhs=self.hT[:, k, sl],
                        start=(k == 0), stop=(k == 7)),
                       reads=[s, self.hT[:, k, sl]], writes=[gps], signal=(k == 7))
                for c in range(2):
                    op("pe", lambda e, pps=pps, c=c, m=m, sl=sl: e.matmul(
                        pps, lhsT=sp[:, c, m * 128:(m + 1) * 128], rhs=self.pT[:, c, sl],
                        start=(c == 0), stop=(c == 1)),
                       reads=[sp, self.pT[:, c, sl]], writes=[pps], signal=(c == 1))
                t = self.tmp()
                op("act", lambda e, t=t, gps=gps: e.activation(out=t, in_=gps, func=AF.Sigmoid),
                   reads=[gps], writes=[t])
                op("dve", lambda e, t=t, pps=pps: e.tensor_tensor(out=t, in0=t, in1=pps, op=ALU.mult),
                   reads=[t, pps], writes=[t])
                op("dve", lambda e, t=t, m=m, sl=sl: e.tensor_tensor(
                    out=self.xT[:, m, sl], in0=t, in1=self.xT[:, m, sl], op=ALU.add),
                   reads=[t, self.xT[:, m, sl]], writes=[self.xT[:, m, sl]])
            self.tail_norm(h)

    def final(self, ti, do_norm=True):
        op, T = self.op, self.T
        if do_norm:
            self.norm("final_norm", 0, self.Yf)
            src = self.Yf
        else:
            src = self.xT
        dst_d = self.y_d[ti * T:(ti + 1) * T, :].rearrange("(g p) d -> p g d", p=128)
        k = 0
        for g in range(self.NG):
            for cq in range(2):
                ps = self.bank(k % 2)
                k += 1
                for c in range(4):
                    cc = cq * 4 + c
                    op("pe", lambda e, ps=ps, g=g, c=c, cc=cc: e.transpose(
                        out=ps[:, c * 128:(c + 1) * 128], in_=src[:, cc, g * 128:(g + 1) * 128],
                        identity=self.ident),
                       reads=[src[:, cc, g * 128:(g + 1) * 128], self.ident], writes=[ps], signal=(c == 3))
                dst = self.Ys[:, g, cq * 512:(cq + 1) * 512]
                if k % 2:
                    op("act", lambda e, dst=dst, ps=ps: e.copy(out=dst, in_=ps), reads=[ps], writes=[dst])
                else:
                    op("dve", lambda e, dst=dst, ps=ps: e.tensor_copy(out=dst, in_=ps), reads=[ps], writes=[dst])
            op("sync", lambda e, g=g: e.dma_start(out=dst_d[:, g, :], in_=self.Ys[:, g, :]),
               reads=[self.Ys[:, g, :]], slot="yo%d" % g)

    def build(self):
        self.declare()
        self.setup()
        self.epsc = self.view(self.alloc(4), [1], F32)
        self.op("dve", lambda e: e.memset(self.epsc, EPS), writes=[self.epsc])
        self.onec = self.view(self.alloc(4), [1], F32)
        self.op("dve", lambda e: e.memset(self.onec, 1.0), writes=[self.onec])
        self.prologue()
        stop = self.stop_after
        for ti in range(self.NT):
            self.load_x(ti)
            phases = []
            for l in range(2):
                phases += [("ffn", l, 0), ("mix", l, 0), ("ffn", l, 1), ("ple", l, 0)]
            done_all = True
            def nparams(ph):
                kind, l, i = ph
                if kind == "ffn":
                    return ("ffn_norm", (l * 2 + i) * 8, self.hT)
                if kind == "mix":
                    return ("mix_norm", l * 8, self.hT)
                return ("ple_norm", l * 8, self.hT)
            for pi, (kind, l, i) in enumerate(phases):
                if stop is not None and pi >= stop:
                    done_all = (stop == -1)
                    break
                if pi + 1 < len(phases):
                    self.next_norm = nparams(phases[pi + 1]) if (stop is None or pi + 1 < stop) else None
                else:
                    self.next_norm = ("final_norm", 0, self.Yf)
                if kind == "ffn":
                    self.ffn(l, i)
                elif kind == "mix":
                    if l == 0:
                        self.mixer0()
                    else:
                        self.mixer1()
                else:
                    self.ple(l, ti)
            self.final(ti, do_norm=done_all)
        self.S.wait_all("sync")
        self.S.emit()

    def run_streams(self, gens):
        real_op = self.S.op
        lists = []
        for g in gens:
            lst = []
            self._rec = lst
            for r in g:
                if r is not None:
                    lst.append(("flag", r[0], r[1]))
            lists.append(lst)
        self._rec = None
        K = Sched.keys
        free = {e: 0.0 for e in ENGS}
        wr = {}
        rd = {}
        flags = set()
        idx = [0] * len(lists)

        def fsize(ap):
            n = 1
            for st_, c in ap.ap[1:]:
                n *= c
            return n

        def est(a, k):
            eng = a[0]
            reads = k.get("reads", ())
            writes = k.get("writes", ())
            t = free[eng]
            rk = [x for ap in reads for x in K(ap)]
            wk = [x for ap in writes for x in K(ap)]
            for x in rk:
                t = max(t, wr.get(x, 0.0))
            for x in wk:
                t = max(t, wr.get(x, 0.0), rd.get(x, 0.0))
            n = fsize(writes[0]) if writes else 64
            if k.get("slot") is not None:
                dur, done = 100.0, 3000.0 + n * 2 / 0.15
            elif eng == "pe":
                dur = max(64.0, n) * 0.42 + 40.0
                done = dur + 150.0
            elif eng == "act":
                dur = 230.0 + 0.83 * n
                done = dur + 100.0
            else:
                dur = 70.0 + 1.04 * n
                done = dur + 100.0
            return t, dur, done, rk, wk

        while True:
            best = None
            alive = False
            for si, lst in enumerate(lists):
                while idx[si] < len(lst) and lst[idx[si]][0] == "flag":
                    _, kind, name = lst[idx[si]]
                    if kind == "set":
                        flags.add(name)
                        idx[si] += 1
                    elif name in flags:
                        idx[si] += 1
                    else:
                        break
                if idx[si] >= len(lst):
                    continue
                alive = True
                if lst[idx[si]][0] == "flag":
                    continue
                _, a, k = lst[idx[si]]
                t, dur, done, rk, wk = est(a, k)
                if best is None or t < best[0]:
                    best = (t, si)
            if best is None:
                assert not alive, "stream deadlock"
                break
            si = best[1]
            lst = lists[si]
            while True:
                _, a, k = lst[idx[si]]
                t, dur, done, rk, wk = est(a, k)
                real_op(*a, **k)
                free[a[0]] = t + dur
                for x in rk:
                    rd[x] = max(rd.get(x, 0.0), t + dur)
                for x in wk:
                    wr[x] = t + done
                    rd[x] = 0.0
                idx[si] += 1
                if k.get("signal", True) or idx[si] >= len(lst) or lst[idx[si]][0] != "op":
                    break

    def fm_proj(self, s, col0, nchunks, dst_fn, chunk0=0, banks=(0, 1)):
        op = self.op
        for ci in range(nchunks):
            for h in range(self.NH):
                sl = slice(h * 512, (h + 1) * 512)
                ps = self.bank(banks[self.pbk % len(banks)])
                self.pbk += 1
                for k in range(8):
                    op("pe", lambda e, ps=ps, k=k, ci=ci, sl=sl: e.matmul(
                        ps, lhsT=s[:, k, col0 + ci * 128:col0 + (ci + 1) * 128], rhs=self.hT[:, k, sl],
                        start=(k == 0), stop=(k == 7)),
                       reads=[s, self.hT[:, k, sl]], writes=[ps], signal=(k == 7))
                dst_fn(chunk0 + ci, h, sl, ps)
                yield

    def tm_proj(self, s, col0, ncols, tg, ps):
        op = self.op
        for k in range(8):
            op("pe", lambda e, k=k: e.matmul(
                ps, lhsT=self.hT[:, k, tg * 128:(tg + 1) * 128], rhs=s[:, k, col0:col0 + ncols],
                start=(k == 0), stop=(k == 7)),
               reads=[s, self.hT[:, k, tg * 128:(tg + 1) * 128]], writes=[ps], signal=(k == 7))

    def decay_item(self, lf, kf, ncols, tg, ngroups, kd_dst, totv, escale, ups, et):
        op = self.op
        op("pe", lambda e: e.matmul(ups, lhsT=self.U, rhs=lf, start=True, stop=True),
           reads=[lf, self.U], writes=[ups])
        for gi in range(ngroups):
            dst = totv[:, gi, 2 * tg:2 * tg + 2]
            op("pe", lambda e, gi=gi, dst=dst: e.matmul(dst, lhsT=lf[:, gi * 128:(gi + 1) * 128], rhs=self.E,
                                                 start=True, stop=True),
               reads=[lf, self.E], writes=[dst], signal=(gi == ngroups - 1))
        op("act", lambda e: e.activation(out=et, in_=ups, func=AF.Exp, scale=escale),
           reads=[ups], writes=[et])
        op("dve", lambda e: e.tensor_tensor(out=kd_dst, in0=kf, in1=et, op=ALU.mult),
           reads=[kf, et], writes=[kd_dst])

    def gla_core(self, items, hpg, kd, qT, kdec, vtm, gT, sgT, hn, sbase, banksets, rhp, pipelined, dstf):
        op, T = self.op, self.T
        sdt = self.stat_dt
        ones = self.ones if sdt == F32 else self.ones_b
        prevs = {}

        def banks_of(it):
            pb, ob, sb_ = banksets[it % len(banksets)]
            PP = self.PS[:, pb * 512:(pb + 2) * 512].rearrange("p (s c v) -> p s c v", s=2, c=4)
            return PP, [self.bank(x) for x in ob], [self.bank(x) for x in sb_]

        def kvmm(it):
            gi, hf = items[it]
            PP, oTs, sts = banks_of(it)
            for c in range(8):
                cc = hf * 8 + c
                tg, sub = cc // 2, cc % 2
                for e_ in range(hpg):
                    head = gi * hpg + e_
                    lhsT = kdec[sub * 64:(sub + 1) * 64, tg, head * kd:(head + 1) * kd]
                    rhs = vtm[sub * 64:(sub + 1) * 64, tg, head * 128:(head + 1) * 128]
                    out = PP[e_ * kd:(e_ + 1) * kd, sub, c // 2, :]
                    op("pe", lambda e, out=out, lhsT=lhsT, rhs=rhs: e.matmul(out, lhsT=lhsT, rhs=rhs,
                                                                       start=True, stop=True),
                       reads=[lhsT, rhs], writes=[out], signal=(e_ == hpg - 1))

        def chain_gen(it, omm):
            gi, hf = items[it]
            PP, oTs, sts = banks_of(it)
            prev = prevs.get(gi, self.Sst[:, sbase + gi, :])
            for c in range(8):
                cc = hf * 8 + c
                sub = cc % 2
                new = self.Str[:, self.str_i % 8, :]
                sb = self.Sb[:, self.str_i % 8, :]
                self.str_i += 1
                pin = PP[:, sub, c // 2, :]
                gcol = gT[:, gi, cc:cc + 1]
                op("dve", lambda e, new=new, prev=prev, pin=pin, gcol=gcol: e.scalar_tensor_tensor(
                    out=new, in0=prev, scalar=gcol, in1=pin, op0=ALU.mult, op1=ALU.add),
                   reads=[prev, pin, gT], writes=[new])
                op("act", lambda e, sb=sb, new=new: e.copy(out=sb, in_=new), reads=[new], writes=[sb])
                for e_ in range(hpg):
                    out = oTs[e_][:, c * 64:(c + 1) * 64]
                    lhsT = sb[e_ * kd:(e_ + 1) * kd, :]
                    rhs = qT[e_ * kd:(e_ + 1) * kd, gi, cc * 64:(cc + 1) * 64]
                    omm.append((out, lhsT, rhs, e_ == hpg - 1))
                prev = new
                yield
            prevs[gi] = prev
            if hf == self.NH - 1:
                dstS = self.Sst[:, sbase + gi, :]
                op("dve", lambda e, dstS=dstS, prev=prev: e.tensor_copy(out=dstS, in_=prev),
                   reads=[prev], writes=[dstS])

        def chain(it):
            omm = []
            for _ in chain_gen(it, omm):
                pass
            return omm

        def chain2(ita, itb):
            oa, ob = [], []
            ga, gb = chain_gen(ita, oa), chain_gen(itb, ob)
            da = db = False
            while not (da and db):
                na, nb = len(oa), len(ob)
                if not da:
                    try:
                        next(ga)
                    except StopIteration:
                        da = True
                if not db:
                    try:
                        next(gb)
                    except StopIteration:
                        db = True
                o_mm(oa[max(0, na - hpg):na] + ob[max(0, nb - hpg):nb])
            return []

        def o_mm(omm):
            for out, lhsT, rhs, sig in omm:
                op("pe", lambda e, out=out, lhsT=lhsT, rhs=rhs: e.matmul(out, lhsT=lhsT, rhs=rhs,
                                                                   start=True, stop=True),
                   reads=[lhsT, rhs], writes=[out], signal=sig)

        hstate = {}

        def h1(it):
            gi, hf = items[it]
            PP, oTs, sts = banks_of(it)
            rhs_ = []
            for e_ in range(hpg):
                head = gi * hpg + e_
                oT = oTs[e_]
                sq = self.sq[head % 2] if sdt == F32 else self.sq[head % 2].bitcast(BF16)[:, 0:512]
                op("act", lambda e, sq=sq, oT=oT: e.activation(out=sq, in_=oT, func=AF.Square),
                   reads=[oT], writes=[sq])
                st = sts[e_]
                op("pe", lambda e, st=st, sq=sq: e.matmul(st, lhsT=ones, rhs=sq, start=True, stop=True),
                   reads=[sq, ones], writes=[st])
                rsb = self.rsb[head % 2]
                op("act", lambda e, st=st, rsb=rsb: e.activation(out=rsb, in_=st, func=AF.Ln, scale=1.0 / 128,
                                                               bias=self.epsc),
                   reads=[st], writes=[rsb])
                rh = rhp[self.rh_i % len(rhp)]
                self.rh_i += 1
                op("act", lambda e, rh=rh, rsb=rsb: e.activation(out=rh, in_=rsb, func=AF.Exp, scale=-0.5),
                   reads=[rsb], writes=[rh])
                rhs_.append(rh)
            hstate[it] = rhs_

        def h2(it):
            gi, hf = items[it]
            sl = slice(hf * 512, (hf + 1) * 512)
            PP, oTs, sts = banks_of(it)
            for e_ in range(hpg):
                head = gi * hpg + e_
                oT = oTs[e_]
                rh = hstate[it][e_]
                op("dve", lambda e, rh=rh, oT=oT: e.tensor_tensor(out=rh, in0=oT, in1=rh, op=ALU.mult),
                   reads=[oT, rh], writes=[rh])
                dst = dstf(head, sl)
                op("dve", lambda e, rh=rh, dst=dst, head=head, sl=sl: e.scalar_tensor_tensor(
                    out=dst, in0=rh, scalar=hn, in1=sgT[:, head, sl], op0=ALU.mult, op1=ALU.mult),
                   reads=[rh, sgT[:, head, sl], self.vecT], writes=[dst])

        n = len(items)
        if pipelined == 2:
            assert n % 2 == 0 and len(banksets) == 2
            npair = n // 2
            kvmm(0)
            kvmm(1)
            for p in range(npair):
                a, b = 2 * p, 2 * p + 1
                if p > 0:
                    h2(a - 2)
                    h2(b - 2)
                omm = chain2(a, b)
                if p + 1 < npair:
                    kvmm(a + 2)
                    kvmm(b + 2)
                o_mm(omm)
                h1(a)
                h1(b)
                yield
            h2(n - 2)
            h2(n - 1)
            yield
        elif pipelined:
            kvmm(0)
            for it in range(n):
                omm = chain(it)
                if it > 0:
                    h2(it - 1)
                if it + 1 < n:
                    kvmm(it + 1)
                o_mm(omm)
                h1(it)
                yield
            h2(n - 1)
            yield
        else:
            for it in range(n):
                kvmm(it)
                omm = chain(it)
                yield
                yield
                o_mm(omm)
                yield
                h1(it)
                h2(it)
                yield

    def mixer1(self):
        op, T, w, NG = self.op, self.T, self.w, self.NG
        self.norm("mix_norm", 8, self.hT)
        o = self.R1
        qT = self.view(o, [8, T], BF16); o += 8 * T * 2
        kdec = self.view(o, [NG, 1024], BF16); o += NG * 2048
        vtm = self.view(o, [NG, 1024], BF16); o += NG * 2048
        sgT = self.view(o, [8, T], F32); o += 8 * T * 4
        gT = self.view(o, [8, 16], F32); o += 512
        rt = []
        while o + 2048 <= self.W2o + 22 * T * 2 and len(rt) < 3:
            rt.append(self.view(o, [512], F32)); o += 2048
        assert len(rt) == 3
        etp = rt[2:3]
        rhp = [self.rstd[:, 0:512], self.rstd[:, 512:1024]]
        tm_ = self.tmps
        fl = [(tm_[0], tm_[1]), (tm_[2], tm_[3]), (rt[0], rt[1])]
        zb = (0, 1, 3)
        self.pbk = 0
        self.str_i = 0
        self.rh_i = 0
        wv = self.wview(w["m1_w_in"][0], 0, 4096)
        totps = self.bank(7)[:, 0:128]
        totv = totps.rearrange("p (g c) -> p g c", g=8)
        gTf = gT.rearrange("p g c -> p (g c)")
        hn = self.vcol("hgrn_head_norm", 0)
        tm = self.tmps

        def P():
            k = 0
            for cb in range(2):
                s = self.wload(wv[:, :, cb * 512:(cb + 1) * 512], [8, 512])
                def evq(ci, h, sl, ps):
                    op("act", lambda e: e.activation(out=qT[:, ci, sl], in_=ps, func=AF.Silu),
                       reads=[ps], writes=[qT[:, ci, sl]])
                yield from self.fm_proj(s, 0, 4, evq, chunk0=cb * 4)
                s = self.wload(wv[:, :, 3072 + cb * 512:3072 + (cb + 1) * 512], [8, 512])
                def evg(ci, h, sl, ps):
                    op("act", lambda e: e.activation(out=sgT[:, ci, sl], in_=ps, func=AF.Silu),
                       reads=[ps], writes=[sgT[:, ci, sl]])
                yield from self.fm_proj(s, 0, 4, evg, chunk0=cb * 4)
            for cb in range(2):
                s = self.wload(wv[:, :, 1024 + cb * 512:1024 + (cb + 1) * 512], [8, 512])
                cols = slice(cb * 512, (cb + 1) * 512)
                pend = []
                for tg in range(NG):
                    ps = self.bank(zb[k % 3])
                    self.tm_proj(s, 0, 512, tg, ps)
                    f, lf = fl[k % 3]
                    op("act", lambda e, f=f, ps=ps: e.activation(out=f, in_=ps, func=AF.Exp, scale=-1.0),
                       reads=[ps], writes=[f])
                    op("dve", lambda e, f=f, lf=lf, cols=cols: e.tensor_tensor(out=lf, in0=f, in1=self.lb_b[:, cols], op=ALU.mult),
                       reads=[f, self.lb_b[:, cols]], writes=[lf])
                    op("act", lambda e, f=f: e.activation(out=f, in_=f, func=AF.Ln, bias=self.onec),
                       reads=[f], writes=[f])
                    op("act", lambda e, lf=lf: e.activation(out=lf, in_=lf, func=AF.Ln, bias=self.onec),
                       reads=[lf], writes=[lf])
                    op("dve", lambda e, f=f, lf=lf: e.tensor_tensor(out=lf, in0=lf, in1=f, op=ALU.subtract),
                       reads=[f, lf], writes=[lf])
                    op("dve", lambda e, f=f, ps=ps: e.tensor_tensor(out=f, in0=f, in1=ps, op=ALU.add),
                       reads=[f, ps], writes=[f])
                    op("act", lambda e, f=f: e.activation(out=f, in_=f, func=AF.Exp, scale=-1.0),
                       reads=[f], writes=[f])
                    op("dve", lambda e, f=f, cols=cols: e.tensor_tensor(out=f, in0=f, in1=self.oml_b[:, cols], op=ALU.mult),
                       reads=[f, self.oml_b[:, cols]], writes=[f])
                    tv = totv[:, cb * 4:(cb + 1) * 4, :]
                    pend.append((lf, f, 512, tg, 4, kdec[:, tg, cols], tv, 1.0, self.bank(2), etp[0]))
                    if len(pend) > 2:
                        self.decay_item(*pend.pop(0))
                    k += 1
                    yield
                while pend:
                    self.decay_item(*pend.pop(0))
                op("act", lambda e, cb=cb: e.activation(out=gTf[:, cb * 64:(cb + 1) * 64],
                                                        in_=totps[:, cb * 64:(cb + 1) * 64], func=AF.Exp),
                   reads=[totps], writes=[gT])
                s = self.wload(wv[:, :, 2048 + cb * 512:2048 + (cb + 1) * 512], [8, 512])
                for tg in range(NG):
                    ps = self.bank(zb[tg % 3])
                    self.tm_proj(s, 0, 512, tg, ps)
                    dst = vtm[:, tg, cols]
                    if tg % 2:
                        op("act", lambda e, dst=dst, ps=ps: e.copy(out=dst, in_=ps), reads=[ps], writes=[dst])
                    else:
                        op("dve", lambda e, dst=dst, ps=ps: e.tensor_copy(out=dst, in_=ps), reads=[ps], writes=[dst])
                    yield
                yield ("set", "cb%d" % cb)

        def dstf(head, sl):
            return qT[:, head, sl]

        def C():
            yield ("wait", "cb0")
            setA = (4, [6], [4])
            itemsA = [(gi, hf) for gi in range(0, 4) for hf in range(self.NH)]
            yield from self.gla_core(itemsA, 1, 128, qT, kdec, vtm, gT, sgT, hn, 2, [setA], rhp, False, dstf)
            yield ("wait", "cb1")
            setA2 = (4, [6], [7])
            setB2 = (0, [2], [3])
            itemsB = []
            for g0 in (4, 6):
                for hf in range(self.NH):
                    itemsB += [(g0, hf), (g0 + 1, hf)]
            yield from self.gla_core(itemsB, 1, 128, qT, kdec, vtm, gT, sgT, hn, 2, [setA2, setB2],
                                     rhp + list(self.tmps[0:2]), 2, dstf)

        self.run_streams([P(), C()])
        self.proj_out("m1_w_out", lambda k, sl: qT[:, k, sl])

    def mixer0(self):
        op, T, w, NG = self.op, self.T, self.w, self.NG
        self.norm("mix_norm", 0, self.hT)
        o = self.R1
        qT = self.view(o, [2, T], BF16); o += 2 * T * 2
        lrT = self.view(o, [T], BF16); o += T * 2
        kdec = self.view(o, [NG, 256], BF16); o += NG * 512
        vtm = self.view(o, [NG, 512], BF16); o += NG * 1024
        sgT = self.view(o, [4, T], F32); o += 4 * T * 4
        ggT = self.view(o, [4, T], F32); o += 4 * T * 4
        xrT = self.view(o, [4, T + 4], F32)
        yb = [self.view(o + c * (T + 4) * 4, [T], BF16) for c in range(4)]
        o += 4 * (T + 4) * 4
        gT = self.view(o, [2, 16], F32); o += 128
        xc = self.view(o, [T], F32); o += T * 4
        xcb = self.view(o, [T], BF16); o += T * 2
        lt = []
        for _ in range(6):
            lt.append(self.view(o, [512], F32)); o += 2048
        rhp = [self.rstd[:, 0:512], self.rstd[:, 512:1024]]
        assert o <= self.W2o + 22 * T * 2, o
        self.pbk = 0
        self.str_i = 0
        self.rh_i = 0
        wv = self.wview(w["m0_w_in"][0], 0, M0C)
        tm = self.tmps
        totps = self.bank(3)[:, 0:32]
        totv = totps.rearrange("p (g c) -> p g c", g=2)
        hn = self.vcol("gla_head_norm", 0)

        def P():
            sB4 = self.wload(wv[:, :, 1552:2064], [8, 512])
            for c in range(4):
                op("dve", lambda e, c=c: e.tensor_copy(out=xrT[:, c, 0:3], in_=self.xr_carry[:, c, 0:3]),
                   reads=[self.xr_carry], writes=[xrT[:, c, 0:3]])
            def evxr(ci, h, sl, ps):
                dst = xrT[:, ci, 3 + h * 512:3 + (h + 1) * 512]
                op("act", lambda e: e.copy(out=dst, in_=ps), reads=[ps], writes=[dst])
            yield from self.fm_proj(sB4, 0, 4, evxr)
            yield ("set", "xr")
            sB5 = self.wload(wv[:, :, 2064:2576], [8, 512])
            def evxg(ci, h, sl, ps):
                t = tm[3]
                op("act", lambda e: e.activation(out=t, in_=ps, func=AF.Square), reads=[ps], writes=[t])
                op("dve", lambda e: e.tensor_scalar(out=t, in0=t, scalar1=0.044715, scalar2=1.0, op0=ALU.mult, op1=ALU.add),
                   reads=[t], writes=[t])
                op("dve", lambda e: e.tensor_tensor(out=t, in0=t, in1=ps, op=ALU.mult), reads=[t, ps], writes=[t])
                op("act", lambda e: e.activation(out=t, in_=t, func=AF.Sigmoid, scale=1.5957691216057308),
                   reads=[t], writes=[t])
                op("dve", lambda e: e.tensor_tensor(out=ggT[:, ci, sl], in0=t, in1=ps, op=ALU.mult),
                   reads=[t, ps], writes=[ggT[:, ci, sl]])
            yield from self.fm_proj(sB5, 0, 4, evxg)
            yield ("set", "gg")
            sB0 = self.wload(wv[:, :, 0:512], [8, 512])
            sLR = self.wload(wv[:, :, 1536:1552], [8, 16])
            sB1 = self.wload(wv[:, :, 512:1024], [8, 512])
            def evq(ci, h, sl, ps):
                op("act", lambda e: e.mul(out=qT[:, ci, sl], in_=ps, mul=0.125), reads=[ps], writes=[qT[:, ci, sl]])
            yield from self.fm_proj(sB0, 0, 2, evq)
            for h in range(self.NH):
                sl = slice(h * 512, (h + 1) * 512)
                ps = self.bank(h % 2)
                for k in range(8):
                    op("pe", lambda e, ps=ps, k=k, sl=sl: e.matmul(ps[0:16, :], lhsT=sLR[:, k, 0:16], rhs=self.hT[:, k, sl],
                                                             start=(k == 0), stop=(k == 7)),
                       reads=[sLR, self.hT[:, k, sl]], writes=[ps], signal=(k == 7))
                op("act", lambda e, ps=ps, sl=sl: e.copy(out=lrT[0:16, sl], in_=ps[0:16, :]), reads=[ps], writes=[lrT[0:16, sl]])
            yield
            pend = None
            for tg in range(NG):
                bk = self.bank(tg % 2)
                lps = bk[:, 0:256]
                kps = self.bank(4)[:, 0:256]
                op("pe", lambda e, lps=lps, tg=tg: e.matmul(lps, lhsT=lrT[0:16, tg * 128:(tg + 1) * 128],
                                                      rhs=self.gup[0:16, :], start=True, stop=True),
                   reads=[lrT[0:16, tg * 128:(tg + 1) * 128], self.gup], writes=[lps])
                sp = tm[tg % 2][:, 0:256]
                op("dve", lambda e, sp=sp, lps=lps: e.tensor_tensor(out=sp, in0=lps, in1=self.gbias_b, op=ALU.add),
                   reads=[lps, self.gbias_b], writes=[sp])
                op("act", lambda e, sp=sp: e.activation(out=sp, in_=sp, func=AF.Exp, scale=-1.0), reads=[sp], writes=[sp])
                op("act", lambda e, sp=sp: e.activation(out=sp, in_=sp, func=AF.Ln, bias=self.onec), reads=[sp], writes=[sp])
                kf = tm[tg % 2][:, 256:512]
                self.tm_proj(sB0, 256, 256, tg, kps)
                op("act", lambda e, kf=kf, kps=kps: e.copy(out=kf, in_=kps), reads=[kps], writes=[kf])
                vps = self.bank(5)
                self.tm_proj(sB1, 0, 512, tg, vps)
                dst = vtm[:, tg, :]
                op("dve", lambda e, dst=dst, vps=vps: e.tensor_copy(out=dst, in_=vps), reads=[vps], writes=[dst])
                if pend is not None:
                    self.decay_item(*pend)
                pend = (sp, kf, 256, tg, 2, kdec[:, tg, :], totv, -1.0 / 16.0, self.bank(2)[:, 0:256], tm[2][:, 0:256])
                yield
            self.decay_item(*pend)
            op("act", lambda e: e.activation(out=gT.rearrange("p g c -> p (g c)"), in_=totps, func=AF.Exp,
                                             scale=-1.0 / 16.0), reads=[totps], writes=[gT])
            sB2 = self.wload(wv[:, :, 1024:1536], [8, 512])
            def evg(ci, h, sl, ps):
                op("act", lambda e: e.activation(out=sgT[:, ci, sl], in_=ps, func=AF.Silu), reads=[ps], writes=[sgT[:, ci, sl]])
            yield from self.fm_proj(sB2, 0, 4, evg)
            yield ("set", "kv")

        def L():
            yield ("wait", "xr")
            yield ("wait", "gg")
            its = [(c, h) for c in range(4) for h in range(self.NH)]

            def conv(c):
                w0 = self.vcol("lru_conv_w", 0 * 4 + c)
                cb_ = self.vcol("lru_conv_b", c)
                op("dve", lambda e: e.tensor_scalar(out=xc, in0=xrT[:, c, 0:T], scalar1=w0, scalar2=cb_,
                                                    op0=ALU.mult, op1=ALU.add),
                   reads=[xrT[:, c, :], self.vecT], writes=[xc])
                for j in range(1, 4):
                    wj = self.vcol("lru_conv_w", j * 4 + c)
                    op("dve", lambda e, j=j, wj=wj: e.scalar_tensor_tensor(
                        out=xc, in0=xrT[:, c, j:j + T], scalar=wj, in1=xc, op0=ALU.mult, op1=ALU.add),
                       reads=[xrT[:, c, :], xc, self.vecT], writes=[xc])
                op("dve", lambda e: e.tensor_copy(out=self.xr_carry[:, c, 0:3], in_=xrT[:, c, T:T + 3]),
                   reads=[xrT[:, c, :]], writes=[self.xr_carry])
                op("act", lambda e: e.copy(out=xcb, in_=xc), reads=[xc], writes=[xcb])

            def stageA(k):
                c, h = its[k]
                sl = slice(h * 512, (h + 1) * 512)
                a_, e2, u_ = lt[3 * (k % 2)], lt[3 * (k % 2) + 1], lt[3 * (k % 2) + 2]
                ra = self.bank(6)
                ia = self.bank(7)
                ba = self.vcol("lru_ba", c)
                bx = self.vcol("lru_bx", c)
                op("pe", lambda e: e.matmul(ra, lhsT=self.bd[:, 0, c, :], rhs=xcb[:, sl], start=True, stop=True),
                   reads=[self.bd, xcb[:, sl]], writes=[ra])
                op("pe", lambda e: e.matmul(ia, lhsT=self.bd[:, 1, c, :], rhs=xcb[:, sl], start=True, stop=True),
                   reads=[self.bd, xcb[:, sl]], writes=[ia])
                op("act", lambda e: e.activation(out=ra, in_=ra, func=AF.Sigmoid, bias=ba),
                   reads=[ra, self.vecT], writes=[ra])
                op("act", lambda e: e.activation(out=u_, in_=ia, func=AF.Sigmoid, bias=bx),
                   reads=[ia, self.vecT], writes=[u_])
                op("act", lambda e: e.activation(out=a_, in_=ra, func=AF.Exp, scale=self.clam[:, c:c + 1]),
                   reads=[ra, self.clam], writes=[a_])
                op("act", lambda e: e.activation(out=e2, in_=ra, func=AF.Exp, scale=self.clam[:, 4 + c:5 + c]),
                   reads=[ra, self.clam], writes=[e2])
                op("dve", lambda e: e.tensor_tensor(out=u_, in0=u_, in1=xc[:, sl], op=ALU.mult),
                   reads=[u_, xc[:, sl]], writes=[u_])

            def stageB(k):
                c, h = its[k]
                sl = slice(h * 512, (h + 1) * 512)
                a_, e2, u_ = lt[3 * (k % 2)], lt[3 * (k % 2) + 1], lt[3 * (k % 2) + 2]
                op("dve", lambda e: e.tensor_scalar(out=e2, in0=e2, scalar1=1.0 - 1e-6, scalar2=-1.0, op0=ALU.min, op1=ALU.mult),
                   reads=[e2], writes=[e2])
                op("act", lambda e: e.activation(out=e2, in_=e2, func=AF.Ln, bias=self.onec), reads=[e2], writes=[e2])
                op("act", lambda e: e.activation(out=e2, in_=e2, func=AF.Exp, scale=0.5), reads=[e2], writes=[e2])
                op("dve", lambda e: e.tensor_tensor(out=u_, in0=u_, in1=e2, op=ALU.mult), reads=[u_, e2], writes=[u_])
                op("dve", lambda e: e.tensor_tensor_scan(out=e2, data0=a_, data1=u_, initial=self.hprev[:, c:c + 1],
                                                        op0=ALU.mult, op1=ALU.add),
                   reads=[a_, u_, self.hprev], writes=[e2])
                op("dve", lambda e: e.tensor_copy(out=self.hprev[:, c:c + 1], in_=e2[:, 511:512]),
                   reads=[e2], writes=[self.hprev])
                dst = yb[c][:, sl]
                op("dve", lambda e: e.tensor_tensor(out=dst, in0=e2, in1=ggT[:, c, sl], op=ALU.mult),
                   reads=[e2, ggT[:, c, sl]], writes=[dst])

            for k in range(len(its)):
                if its[k][1] == 0:
                    conv(its[k][0])
                    yield
                    yield
                    yield
                    yield
                stageA(k)
                yield
                if k > 0:
                    stageB(k - 1)
                    yield
            stageB(len(its) - 1)

        def C():
            yield ("wait", "kv")
            items = [(gi, hf) for gi in range(2) for hf in range(self.NH)]
            setA = (0, [2, 3], [0, 1])
            yield from self.gla_core(items, 2, 64, qT, kdec, vtm, gT, sgT, hn, 0, [setA], rhp, False,
                                     lambda head, sl: self.hT[:, head, sl])

        self.run_streams([P(), L(), C()])
        self.proj_out("m0_w_out", lambda k, sl: (self.hT[:, k, sl] if k < 4 else yb[k - 4][:, sl]))


def build_nc(T=1024, NT=4, stop_after=None, stat_dt=BF16):
    nc = bass.Bass("TRN2", target_bir_lowering=False)
    with ExitStack() as st:
        b = Builder(nc, st, T=T, NT=NT, stop_after=stop_after, stat_dt=stat_dt)
        b.build()
        info = (b.S.nops, b.S.nwaits, b.aoff)
    return nc, info


def kernel(**inputs):
    n = 8
    nc, info = build_nc()
    consts = make_consts()
    x = np.ascontiguousarray(inputs["x"], dtype=np.float32)
    p = np.ascontiguousarray(inputs["p"], dtype=np.float32)
    wts = {k: np.ascontiguousarray(inputs[k], dtype=np.float32) for k in W_SHAPES}
    in_maps = []
    for c in range(n):
        m = {"x": x[c], "p": np.ascontiguousarray(p[:, c]), "consts": consts}
        m.update(wts)
        in_maps.append(m)
    res = run_bass_kernel_spmd(nc, in_maps, core_ids=list(range(n)))
    return np.stack([r["y"] for r in res.results], axis=0)
```
